# Optimizing a Trainium2 kernel written in Bass

```python
import math
import jax, jax.numpy as jnp
from jax import lax
import numpy as np

D_MODEL = 4096
BATCH = 4
SEQ = 2048
DEPTH = 2
DEC_BATCH = 8
DEC_SEQ = 8
PAST_LEN = 16384
PAGE_SIZE = 128

D_MIX = D_MODEL // 2
N_BRANCH = 3
SSD_HEADDIM = 64
SSD_HEADS = D_MIX // SSD_HEADDIM
SSD_GROUPS = 4
SSD_STATE = 128
SSD_CONV = 4
SSD_CHUNK = 128
SSD_XBC = D_MIX + 2 * SSD_GROUPS * SSD_STATE
CONV_WIDTH = 31
CONV_CH = D_MIX
ATT_HEAD_DIM = 128
ATT_HEADS = D_MIX // ATT_HEAD_DIM
ATT_KV_HEADS = 4
MOBA_BLOCK = 256
MOBA_TOPK = 3
Q_CHUNK = 8
ROPE_THETA = 10000.0
D_FF = 4 * D_MODEL
EPS = 1e-6
N_IN = (D_MIX + SSD_XBC + SSD_HEADS) + 2 * CONV_CH + (ATT_HEADS + 2 * ATT_KV_HEADS) * ATT_HEAD_DIM + N_BRANCH * D_MODEL

kernel_name = 'hybrid_ssd_conformer_moba_step'


def _rmsnorm(x, g):
    xf = x.astype(jnp.float32)
    y = xf * lax.rsqrt(jnp.mean(xf * xf, axis=-1, keepdims=True) + EPS)
    return (y * g.astype(jnp.float32)).astype(x.dtype)


def _layernorm(x, g, b):
    xf = x.astype(jnp.float32)
    mu = jnp.mean(xf, axis=-1, keepdims=True)
    var = jnp.mean(jnp.square(xf - mu), axis=-1, keepdims=True)
    y = (xf - mu) * lax.rsqrt(var + EPS)
    return (y * g.astype(jnp.float32) + b.astype(jnp.float32)).astype(x.dtype)


def _causal_dwconv(u, buf, w, b):
    width = w.shape[0]
    ext = jnp.concatenate([buf.astype(u.dtype), u], axis=1)
    out = lax.conv_general_dilated(ext, w[:, None, :].astype(ext.dtype), window_strides=(1,),
                                   padding='VALID', dimension_numbers=('NWC', 'WIO', 'NWC'),
                                   feature_group_count=u.shape[-1])
    return out + b.astype(out.dtype), ext[:, ext.shape[1] - (width - 1):]


def _rope(x, pos):
    half = x.shape[-1] // 2
    inv_freq = jnp.exp(-math.log(ROPE_THETA) * jnp.arange(half, dtype=jnp.float32) * 2.0 / x.shape[-1])
    ang = pos[:, None] * inv_freq[None, :]
    cos = jnp.cos(ang)[:, None, :]
    sin = jnp.sin(ang)[:, None, :]
    xf = x.astype(jnp.float32)
    x1, x2 = xf[..., :half], xf[..., half:]
    return jnp.concatenate([x1 * cos - x2 * sin, x2 * cos + x1 * sin], axis=-1).astype(x.dtype)


def _ssd(x, dt, a, bm, cm, h0):
    b_, l_ = x.shape[:2]
    t_c = SSD_CHUNK if l_ % SSD_CHUNK == 0 else l_
    nc = l_ // t_c
    r = SSD_HEADS // SSD_GROUPS
    xc = x.astype(jnp.float32).reshape(b_, nc, t_c, SSD_GROUPS, r, SSD_HEADDIM)
    dtc = dt.reshape(b_, nc, t_c, SSD_GROUPS, r)
    bc = bm.astype(jnp.float32).reshape(b_, nc, t_c, SSD_GROUPS, SSD_STATE)
    cc = cm.astype(jnp.float32).reshape(b_, nc, t_c, SSD_GROUPS, SSD_STATE)
    acum = jnp.cumsum(dtc * a.reshape(SSD_GROUPS, r), axis=2)
    seg = acum[:, :, :, None] - acum[:, :, None, :]
    causal = jnp.tril(jnp.ones((t_c, t_c), dtype=bool))[:, :, None, None]
    decay = jnp.exp(jnp.where(causal, seg, -jnp.inf))
    cb = jnp.einsum('bctgn,bcsgn->bctsg', cc, bc)
    wts = cb[..., None] * decay * dtc[:, :, None]
    y_diag = jnp.einsum('bctsgr,bcsgrp->bctgrp', wts, xc)
    to_end = jnp.exp(acum[:, :, -1:] - acum) * dtc
    states = jnp.einsum('bctgn,bctgr,bctgrp->bcgrpn', bc, to_end, xc)
    chunk_decay = jnp.exp(acum[:, :, -1])

    def step(h, inp):
        st, dec = inp
        return h * dec[..., None, None] + st, h

    h_init = h0.astype(jnp.float32).reshape(b_, SSD_GROUPS, r, SSD_HEADDIM, SSD_STATE)
    h_fin, h_in = lax.scan(step, h_init, (jnp.moveaxis(states, 1, 0), jnp.moveaxis(chunk_decay, 1, 0)))
    h_in = jnp.moveaxis(h_in, 0, 1)
    y_off = jnp.einsum('bctgn,bcgrpn->bctgrp', cc, h_in) * jnp.exp(acum)[..., None]
    y = (y_diag + y_off).reshape(b_, l_, SSD_HEADS, SSD_HEADDIM)
    return y, h_fin.reshape(b_, SSD_HEADS, SSD_HEADDIM, SSD_STATE)


def _moba(q, k_all, v_all, qpos):
    b, t_len = q.shape[:2]
    lk = k_all.shape[1]
    nb = -(-lk // MOBA_BLOCK)
    pad = nb * MOBA_BLOCK - lk
    padw = ((0, 0), (0, pad), (0, 0), (0, 0))
    kb = jnp.pad(k_all, padw).reshape(b, nb, MOBA_BLOCK, ATT_KV_HEADS, ATT_HEAD_DIM).transpose(0, 3, 1, 2, 4)
    vb = jnp.pad(v_all, padw).reshape(b, nb, MOBA_BLOCK, ATT_KV_HEADS, ATT_HEAD_DIM).transpose(0, 3, 1, 2, 4)
    head_kv = jnp.arange(ATT_HEADS) // (ATT_HEADS // ATT_KV_HEADS)
    kmean = jnp.mean(kb.astype(jnp.float32), axis=3)[:, head_kv]
    n_sel = min(MOBA_TOPK, nb)
    qc = Q_CHUNK if t_len % Q_CHUNK == 0 else t_len
    nc = t_len // qc
    scale = ATT_HEAD_DIM ** -0.5
    bi = jnp.arange(b)[:, None, None, None]
    hi = head_kv[None, :, None, None]
    offs = jnp.arange(MOBA_BLOCK, dtype=jnp.int32)
    blk_ids = jnp.arange(nb, dtype=jnp.int32)

    def chunk(args):
        qq, pp = args
        qf = qq.astype(jnp.float32)
        qblk = pp // MOBA_BLOCK
        gate = jnp.einsum('bhqd,bhnd->bhqn', qf, kmean)
        gate = jnp.where(blk_ids[None, :] < qblk[:, None], gate, -jnp.inf)
        top_val, top_idx = lax.top_k(gate, n_sel)
        own = jnp.broadcast_to(qblk[None, None, :, None], top_idx.shape[:3] + (1,)).astype(top_idx.dtype)
        idx = jnp.concatenate([top_idx, own], axis=-1)
        slot_ok = jnp.concatenate([top_val > -jnp.inf, jnp.ones(own.shape, dtype=bool)], axis=-1)
        kg = kb[bi, hi, idx]
        vg = vb[bi, hi, idx]
        kpos = idx[..., None] * MOBA_BLOCK + offs
        mask = slot_ok[..., None] & (kpos <= pp[None, None, :, None, None])
        s = jnp.einsum('bhqd,bhqskd->bhqsk', qf, kg.astype(jnp.float32)) * scale
        s = jnp.where(mask, s, -jnp.inf).reshape(b, ATT_HEADS, qq.shape[2], -1)
        pr = jax.nn.softmax(s, axis=-1).reshape(mask.shape)
        return jnp.einsum('bhqsk,bhqskd->bhqd', pr, vg.astype(jnp.float32))

    qs = jnp.moveaxis(q.transpose(0, 2, 1, 3).reshape(b, ATT_HEADS, nc, qc, ATT_HEAD_DIM), 2, 0)
    ps = qpos.reshape(nc, qc)
    out = lax.map(chunk, (qs, ps))
    out = jnp.moveaxis(out, 0, 2).reshape(b, ATT_HEADS, t_len, ATT_HEAD_DIM)
    return out.transpose(0, 2, 1, 3).reshape(b, t_len, ATT_HEADS * ATT_HEAD_DIM)


def _layer(x, pos0, k_past, v_past, ssm_h0, ssm_buf0, conv_buf0, p):
    b, t_len, _ = x.shape
    h = _rmsnorm(x, p['norm_mix'])
    widths = (D_MIX, SSD_XBC, SSD_HEADS, 2 * CONV_CH, ATT_HEADS * ATT_HEAD_DIM,
              ATT_KV_HEADS * ATT_HEAD_DIM, ATT_KV_HEADS * ATT_HEAD_DIM, N_BRANCH * D_MODEL)
    cuts = [sum(widths[:i + 1]) for i in range(len(widths) - 1)]
    z, xbc, dt_raw, conv_in, q, k, v, gates = jnp.split(h @ p['w_in'], cuts, axis=-1)

    xbc, ssm_buf = _causal_dwconv(xbc, ssm_buf0, p['ssd_conv_w'], p['ssd_conv_b'])
    xbc = jax.nn.silu(xbc)
    nbc = SSD_GROUPS * SSD_STATE
    xs = xbc[..., :D_MIX].reshape(b, t_len, SSD_HEADS, SSD_HEADDIM)
    bm = xbc[..., D_MIX:D_MIX + nbc].reshape(b, t_len, SSD_GROUPS, SSD_STATE)
    cm = xbc[..., D_MIX + nbc:].reshape(b, t_len, SSD_GROUPS, SSD_STATE)
    dt = jax.nn.softplus(dt_raw.astype(jnp.float32) + p['ssd_dt_bias'].astype(jnp.float32))
    a = -jnp.exp(p['ssd_a_log'].astype(jnp.float32))
    y, ssm_h = _ssd(xs, dt, a, bm, cm, ssm_h0)
    y = y + p['ssd_d'].astype(jnp.float32)[:, None] * xs.astype(jnp.float32)
    y = y.reshape(b, t_len, D_MIX) * jax.nn.silu(z.astype(jnp.float32))
    yg = y.reshape(b, t_len, SSD_GROUPS, D_MIX // SSD_GROUPS)
    yg = yg * lax.rsqrt(jnp.mean(yg * yg, axis=-1, keepdims=True) + EPS)
    y = yg.reshape(b, t_len, D_MIX) * p['ssd_norm'].astype(jnp.float32)
    y_ssd = y.astype(x.dtype) @ p['w_ssd_out']

    u_val, u_gate = jnp.split(conv_in, 2, axis=-1)
    u = u_val * jax.nn.sigmoid(u_gate)
    u, conv_buf = _causal_dwconv(u, conv_buf0, p['conv_w'], p['conv_b'])
    u = jax.nn.silu(_layernorm(u, p['conv_ln_g'], p['conv_ln_b']))
    y_conv = u @ p['w_conv_out']

    posf = pos0 + jnp.arange(t_len, dtype=jnp.float32)
    qpos = pos0 + jnp.arange(t_len, dtype=jnp.int32)
    q = _rope(q.reshape(b, t_len, ATT_HEADS, ATT_HEAD_DIM), posf)
    k = _rope(k.reshape(b, t_len, ATT_KV_HEADS, ATT_HEAD_DIM), posf)
    v = v.reshape(b, t_len, ATT_KV_HEADS, ATT_HEAD_DIM)
    att = _moba(q, jnp.concatenate([k_past.astype(k.dtype), k], axis=1),
                jnp.concatenate([v_past.astype(v.dtype), v], axis=1), qpos)
    y_att = att.astype(x.dtype) @ p['w_att_out']

    g = jax.nn.sigmoid(gates.astype(jnp.float32)).astype(x.dtype)
    g_ssd, g_conv, g_att = jnp.split(g, N_BRANCH, axis=-1)
    x = x + (g_ssd * y_ssd + g_conv * y_conv + g_att * y_att) @ p['w_o']
    hf = _rmsnorm(x, p['norm_ffn'])
    x = x + jnp.square(jax.nn.relu(hf @ p['w_up'])) @ p['w_down']
    return x, (k, v, ssm_h, ssm_buf, conv_buf)


def setup_inputs(seed: int = 0) -> dict:
    key = jax.random.key(seed)
    ks = jax.random.split(key, 32)
    n_pages = PAST_LEN // PAGE_SIZE
    n_phys = (DEC_BATCH * n_pages * 5) // 4

    def nrm(k, shape, s):
        return jax.random.normal(k, shape, jnp.float32) * s

    dt0 = jnp.exp(jax.random.uniform(ks[10], (DEPTH, SSD_HEADS), jnp.float32,
                                     minval=math.log(1e-3), maxval=math.log(1e-1)))
    page_table = jax.random.permutation(ks[4], n_phys)[:DEC_BATCH * n_pages].reshape(DEC_BATCH, n_pages).astype(jnp.int32)
    return {
        'x_prompt': nrm(ks[0], (BATCH, SEQ, D_MODEL), 1.0),
        'x_sample': nrm(ks[1], (DEC_BATCH, DEC_SEQ, D_MODEL), 1.0),
        'cache_k': nrm(ks[2], (DEPTH, n_phys, PAGE_SIZE, ATT_KV_HEADS, ATT_HEAD_DIM), 1.0),
        'cache_v': nrm(ks[3], (DEPTH, n_phys, PAGE_SIZE, ATT_KV_HEADS, ATT_HEAD_DIM), 1.0),
        'page_table': page_table,
        'state_ssm': nrm(ks[5], (DEPTH, DEC_BATCH, SSD_HEADS, SSD_HEADDIM, SSD_STATE), 0.1),
        'state_ssm_conv': nrm(ks[6], (DEPTH, DEC_BATCH, SSD_CONV - 1, SSD_XBC), 1.0),
        'state_conv': nrm(ks[7], (DEPTH, DEC_BATCH, CONV_WIDTH - 1, CONV_CH), 0.5),
        'norm_mix': 1.0 + nrm(ks[8], (DEPTH, D_MODEL), 0.02),
        'w_in': nrm(ks[9], (DEPTH, D_MODEL, N_IN), D_MODEL ** -0.5),
        'ssd_conv_w': nrm(ks[11], (DEPTH, SSD_CONV, SSD_XBC), SSD_CONV ** -0.5),
        'ssd_conv_b': nrm(ks[12], (DEPTH, SSD_XBC), 0.02),
        'ssd_dt_bias': dt0 + jnp.log(-jnp.expm1(-dt0)),
        'ssd_a_log': jnp.log(jax.random.uniform(ks[13], (DEPTH, SSD_HEADS), jnp.float32, minval=1.0, maxval=16.0)),
        'ssd_d': 1.0 + nrm(ks[14], (DEPTH, SSD_HEADS), 0.1),
        'ssd_norm': 1.0 + nrm(ks[15], (DEPTH, D_MIX), 0.02),
        'w_ssd_out': nrm(ks[16], (DEPTH, D_MIX, D_MODEL), D_MIX ** -0.5),
        'conv_w': nrm(ks[17], (DEPTH, CONV_WIDTH, CONV_CH), CONV_WIDTH ** -0.5),
        'conv_b': nrm(ks[18], (DEPTH, CONV_CH), 0.02),
        'conv_ln_g': 1.0 + nrm(ks[19], (DEPTH, CONV_CH), 0.02),
        'conv_ln_b': nrm(ks[20], (DEPTH, CONV_CH), 0.02),
        'w_conv_out': nrm(ks[21], (DEPTH, CONV_CH, D_MODEL), CONV_CH ** -0.5),
        'w_att_out': nrm(ks[22], (DEPTH, ATT_HEADS * ATT_HEAD_DIM, D_MODEL), (ATT_HEADS * ATT_HEAD_DIM) ** -0.5),
        'w_o': nrm(ks[23], (DEPTH, D_MODEL, D_MODEL), D_MODEL ** -0.5),
        'norm_ffn': 1.0 + nrm(ks[24], (DEPTH, D_MODEL), 0.02),
        'w_up': nrm(ks[25], (DEPTH, D_MODEL, D_FF), D_MODEL ** -0.5),
        'w_down': nrm(ks[26], (DEPTH, D_FF, D_MODEL), D_FF ** -0.5),
        'norm_final': 1.0 + nrm(ks[27], (D_MODEL,), 0.02),
    }


def reference(x_prompt, x_sample, cache_k, cache_v, page_table, state_ssm, state_ssm_conv, state_conv,
              norm_mix, w_in, ssd_conv_w, ssd_conv_b, ssd_dt_bias, ssd_a_log, ssd_d, ssd_norm, w_ssd_out,
              conv_w, conv_b, conv_ln_g, conv_ln_b, w_conv_out, w_att_out, w_o, norm_ffn, w_up, w_down,
              norm_final):
    n_pages = PAST_LEN // PAGE_SIZE
    xp, xs = x_prompt, x_sample
    st_p, st_s = [], []
    for l in range(DEPTH):
        p = {'norm_mix': norm_mix[l], 'w_in': w_in[l], 'ssd_conv_w': ssd_conv_w[l], 'ssd_conv_b': ssd_conv_b[l],
             'ssd_dt_bias': ssd_dt_bias[l], 'ssd_a_log': ssd_a_log[l], 'ssd_d': ssd_d[l], 'ssd_norm': ssd_norm[l],
             'w_ssd_out': w_ssd_out[l], 'conv_w': conv_w[l], 'conv_b': conv_b[l], 'conv_ln_g': conv_ln_g[l],
             'conv_ln_b': conv_ln_b[l], 'w_conv_out': w_conv_out[l], 'w_att_out': w_att_out[l], 'w_o': w_o[l],
             'norm_ffn': norm_ffn[l], 'w_up': w_up[l], 'w_down': w_down[l]}
        bp = xp.shape[0]
        kv0 = jnp.zeros((bp, 0, ATT_KV_HEADS, ATT_HEAD_DIM), xp.dtype)
        xp, sp = _layer(xp, 0, kv0, kv0,
                        jnp.zeros((bp, SSD_HEADS, SSD_HEADDIM, SSD_STATE), jnp.float32),
                        jnp.zeros((bp, SSD_CONV - 1, SSD_XBC), xp.dtype),
                        jnp.zeros((bp, CONV_WIDTH - 1, CONV_CH), xp.dtype), p)
        k_past = cache_k[l][page_table].reshape(DEC_BATCH, n_pages * PAGE_SIZE, ATT_KV_HEADS, ATT_HEAD_DIM)
        v_past = cache_v[l][page_table].reshape(DEC_BATCH, n_pages * PAGE_SIZE, ATT_KV_HEADS, ATT_HEAD_DIM)
        xs, ss = _layer(xs, PAST_LEN, k_past, v_past, state_ssm[l], state_ssm_conv[l], state_conv[l], p)
        st_p.append(sp)
        st_s.append(ss)
    y_prompt = _rmsnorm(xp, norm_final)
    y_sample = _rmsnorm(xs, norm_final)
    k_prompt = jnp.stack([s[0] for s in st_p])
    v_prompt = jnp.stack([s[1] for s in st_p])
    k_sample = jnp.stack([s[0] for s in st_s])
    v_sample = jnp.stack([s[1] for s in st_s])
    ssm_prompt = jnp.stack([s[2] for s in st_p])
    ssm_sample = jnp.stack([s[2] for s in st_s])
    ssm_conv_prompt = jnp.stack([s[3] for s in st_p])
    ssm_conv_sample = jnp.stack([s[3] for s in st_s])
    conv_prompt = jnp.stack([s[4] for s in st_p])
    conv_sample = jnp.stack([s[4] for s in st_s])
    return (y_prompt, y_sample, k_prompt, v_prompt, k_sample, v_sample, ssm_prompt, ssm_sample,
            ssm_conv_prompt, ssm_conv_sample, conv_prompt, conv_sample)
```

```python
import math
from contextlib import ExitStack

import numpy as np
import concourse.bass as bass
import concourse.mybir as mybir
from concourse.bass_utils import run_bass_kernel_spmd

F32 = mybir.dt.float32
BF16 = mybir.dt.bfloat16
I32 = mybir.dt.int32
ALU = mybir.AluOpType
AF = mybir.ActivationFunctionType
AX = mybir.AxisListType
EPS = 1e-6
NEG = -30000.0


class Cfg:
    def __init__(s, D=4096, SEQ=2048, PAST=16384, L=2):
        s.D = D; s.DM = D // 2; s.KC = D // 128; s.KCM = s.DM // 128
        s.NH = s.DM // 64; s.R = s.NH // 4; s.XBC = s.DM + 1024; s.XT = s.XBC // 128
        s.CC = s.DM; s.CT = s.CC // 128
        s.AH = s.DM // 128; s.AR = s.AH // 4; s.RW = s.AR * 8
        s.DFF = 4 * D; s.GS = s.DM // 4
        s.oz = 0; s.oxbc = s.DM; s.odt = s.oxbc + s.XBC; s.oconv = s.odt + s.NH
        s.oq = s.oconv + 2 * s.CC; s.ok = s.oq + s.DM; s.ov = s.ok + 512; s.og = s.ov + 512
        s.NIN = s.og + 3 * D
        s.SEQ = SEQ; s.PAST = PAST; s.L = L; s.NPG = PAST // 128
        s.NPHYS = (8 * s.NPG * 5) // 4
        s.NBP = SEQ // 256; s.NBS = PAST // 256
        s.NBSP = max(8, s.NBS)


class Prog:
    ENG = ('pe', 'act', 'dve', 'pool', 'sp')

    def __init__(self, nc, stack, nlanes=6):
        self.nc = nc
        self.sem = {}
        self.count = {}
        self.ops = {e: [] for e in self.ENG}
        self.seen = {e: {} for e in self.ENG}
        self.unsig = {e: False for e in self.ENG}
        for e in self.ENG:
            self.sem[e] = stack.enter_context(nc.semaphore('s_' + e))
            self.count[e] = 0
        self.lanes = {}
        self.rr = {}
        for q in ('sp', 'pool', 'act'):
            self.lanes[q] = []
            for i in range(nlanes):
                k = 'l_%s%d' % (q, i)
                self.sem[k] = stack.enter_context(nc.semaphore(k))
                self.count[k] = 0
                self.lanes[q].append(k)
            self.rr[q] = 0
        self.buf = {}
        self.nbank = 0
        self.nbbank = 0
        self.nslot = 0

    def _deps(self, r, w):
        deps = set()
        for b in r:
            st = self.buf.get(b)
            if st and st['w']:
                deps.add(st['w'])
        for b in w:
            st = self.buf.get(b)
            if st:
                if st['w']:
                    deps.add(st['w'])
                deps.update(st['r'])
        return deps

    def _waits(self, e, deps):
        best = {}
        for (k, v) in deps:
            if k == 'pe' and e == 'pe':
                continue
            if self.seen[e].get(k, 0) >= v:
                continue
            if best.get(k, 0) < v:
                best[k] = v
        for k, v in best.items():
            self.seen[e][k] = v
        return list(best.items())

    def _commit(self, tok, r, w):
        for b in r:
            self.buf.setdefault(b, {'w': None, 'r': []})['r'].append(tok)
        for b in w:
            self.buf[b] = {'w': tok, 'r': []}

    def op(self, e, fn, r=(), w=(), signal=True):
        assert signal or e == 'pe'
        deps = self._deps(r, w)
        waits = self._waits(e, deps)
        if signal:
            self.count[e] += 1
            tok = (e, self.count[e])
            self.unsig[e] = False
        else:
            tok = (e, self.count[e] + 1)
            self.unsig[e] = True
        self.ops[e].append((waits, fn, (e, 1) if signal else None))
        self._commit(tok, r, w)
        return tok

    def dma(self, q, fn, r=(), w=()):
        lanes = self.lanes[q]
        lk = lanes[self.rr[q] % len(lanes)]
        self.rr[q] += 1
        deps = self._deps(r, w)
        if self.count[lk] > 0:
            deps.add((lk, self.count[lk]))
        waits = self._waits(q, deps)
        self.count[lk] += 16
        tok = (lk, self.count[lk])
        self.ops[q].append((waits, fn, (lk, 16)))
        self._commit(tok, r, w)
        return tok

    def barrier(self):
        for e in self.ENG:
            assert not self.unsig[e]
        toks = set()
        for k, v in self.count.items():
            if v > 0:
                toks.add((k, v))
        for e in self.ENG:
            waits = self._waits(e, toks)
            if waits:
                self.ops[e].append((waits, None, None))
        self.buf = {}

    def bank(self):
        b = self.nbank % 6
        self.nbank += 1
        return b

    def bbank(self):
        b = self.nbbank % 2
        self.nbbank += 1
        return b

    def emit(self):
        nc = self.nc

        def run(e, h):
            for waits, fn, inc in self.ops[e]:
                for k, v in waits:
                    h.wait_ge(self.sem[k], v)
                if fn is None:
                    continue
                ins = fn(h)
                if inc is not None:
                    ins.then_inc(self.sem[inc[0]], inc[1])

        with nc.Block() as block:
            @block.tensor
            def _(h):
                run('pe', h)

            @block.scalar
            def _(h):
                run('act', h)

            @block.vector
            def _(h):
                run('dve', h)

            @block.gpsimd
            def _(h):
                run('pool', h)

            @block.sync
            def _(h):
                run('sp', h)


def bc(ap, axis, size):
    u = ap.unsqueeze(axis)
    shp = list(u.shape)
    shp[axis] = size
    return u.broadcast_to(shp)


def build(c, TSEG=1):
    nc = bass.Bass("TRN2", target_bir_lowering=False)
    D, DM, KC, KCM, NH, R, XBC, XT = c.D, c.DM, c.KC, c.KCM, c.NH, c.R, c.XBC, c.XT
    CC, CT, AH, AR, RW, DFF, GS, L, SEQ = c.CC, c.CT, c.AH, c.AR, c.RW, c.DFF, c.GS, c.L, c.SEQ
    NPG, NBP, NBS, NBSP = c.NPG, c.NBP, c.NBS, c.NBSP
    TP = 128 * TSEG
    TM = TP + 8
    NSEG = TSEG + 1
    NT = SEQ // TP
    NROWS = c.NPHYS * 128
    SCALE = 128.0 ** -0.5
    HP = min(R, 4)

    def din(name, shape, dt=F32):
        return nc.dram_tensor(name, list(shape), dt, kind="ExternalInput").ap()

    def dout(name, shape):
        return nc.dram_tensor(name, list(shape), F32, kind="ExternalOutput").ap()

    xp = din("xp", [SEQ, D]); xs = din("xs", [8, D])
    ck = [din("ck%d" % l, [NROWS, 512]) for l in range(L)]
    cv = [din("cv%d" % l, [NROWS, 512]) for l in range(L)]
    pt = din("pt", [1, NPG], I32)
    i_ssmT = din("i_ssmT", [L, 128, DM]); i_xtail = din("i_xtail", [L, 128, XT, 3])
    i_utail = din("i_utail", [L, 128, CT, 30]); i_convrows = din("i_convrows", [L, 30, CC])
    w_in = din("w_in", [L, D, c.NIN]); w_ssd = din("w_ssd", [L, DM, D]); w_conv = din("w_conv", [L, DM, D])
    w_att = din("w_att", [L, DM, D]); w_o = din("w_o", [L, D, D]); w_up = din("w_up", [L, D, DFF])
    w_down = din("w_down", [L, DFF, D])
    p_nmix = din("p_nmix", [L, 128, KC]); p_nffn = din("p_nffn", [L, 128, KC]); p_nfin = din("p_nfin", [1, D])
    p_scw = din("p_scw", [L, 128, XT, 4]); p_scb = din("p_scb", [L, 128, XT])
    p_dtb = din("p_dtb", [L, 1, NH]); p_alog = din("p_alog", [L, 1, NH]); p_sd = din("p_sd", [L, 1, NH])
    p_snorm = din("p_snorm", [L, 128, KCM])
    p_cw = din("p_cw", [L, 128, CT, 31]); p_cb = din("p_cb", [L, 128, CT])
    p_lng = din("p_lng", [L, 128, CT]); p_lnb = din("p_lnb", [L, 128, CT])
    NCONST = 128 * 4 + 1 + 8
    consts = din("consts", [128, NCONST])
    rope_p = din("rope_p", [SEQ, 128]); rope_s = din("rope_s", [8, 128])

    y_p = dout("y_p", [SEQ, D]); y_s = dout("y_s", [8, D])
    k_p = dout("k_p", [L, SEQ, 512]); v_p = dout("v_p", [L, SEQ, 512])
    k_s = dout("k_s", [L, 8, 512]); v_s = dout("v_s", [L, 8, 512])
    o_ssm_p = dout("o_ssm_p", [L, 128, DM]); o_ssm_s = dout("o_ssm_s", [L, 128, DM])
    o_sconv_p = dout("o_sconv_p", [L, 3, XBC]); o_sconv_s = dout("o_sconv_s", [L, 3, XBC])
    o_conv_p = dout("o_conv_p", [L, 30, CC]); o_conv_s = dout("o_conv_s", [L, 30, CC])
    wsrc = {'in': w_in, 'ssd': w_ssd, 'conv': w_conv, 'att': w_att, 'o': w_o, 'up': w_up, 'down': w_down}
    Wb = {}
    for l in range(L):
        for nm, wfull in wsrc.items():
            K_, N_ = wfull.shape[1], wfull.shape[2]
            Wb[(nm, l)] = nc.dram_tensor("wb_%s%d" % (nm, l), [K_, N_], BF16).ap()
    xres_p = nc.dram_tensor("xres_p", [SEQ, D], F32).ap()
    xres_s = nc.dram_tensor("xres_s", [8, D], F32).ap()

    with ExitStack() as st:
        P = Prog(nc, st)

        sbn = [0]

        def sb(name, shape, dt=F32, stack=None):
            sbn[0] += 1
            return (stack or st).enter_context(nc.sbuf_tensor("%s_%d" % (name, sbn[0]), list(shape), dt))

        ps = st.enter_context(nc.psum_tensor("ps", [128, 6, 512], F32))
        pb = st.enter_context(nc.psum_tensor("pb", [128, 2, 1024], BF16))

        def mm(out, lhsT, rhs, start, stop, r, w, signal=False):
            P.op('pe', lambda e: e.matmul(out, lhsT, rhs, start=start, stop=stop), r, w, signal)

        def tr(out, in_, ident, r, w, signal=True):
            P.op('pe', lambda e: e.transpose(out, in_, ident), r, w, signal)

        def act(out, in_, func, r, w, **kw):
            P.op('act', lambda e: e.activation(out, in_, func, **kw), r, w)

        def tt(out, in0, in1, op, r, w, eng='dve'):
            P.op(eng, lambda e: e.tensor_tensor(out, in0, in1, op), r, w)

        def ts(out, in0, s1, s2, op0, op1, r, w, eng='dve'):
            if s2 is None:
                P.op(eng, lambda e: e.tensor_scalar(out, in0, s1, None, op0), r, w)
            else:
                P.op(eng, lambda e: e.tensor_scalar(out, in0, s1, s2, op0, op1), r, w)

        def stt(out, in0, scalar, in1, op0, op1, r, w, eng='dve'):
            P.op(eng, lambda e: e.scalar_tensor_tensor(out, in0, scalar, in1, op0, op1), r, w)

        def cp(out, in_, r, w, eng='dve'):
            if eng == 'act':
                P.op('act', lambda e: e.copy(out, in_), r, w)
            else:
                P.op(eng, lambda e: e.tensor_copy(out, in_), r, w)

        def red(out, in_, op, r, w):
            P.op('dve', lambda e: e.tensor_reduce(out, in_, AX.X, op), r, w)

        def rsqrt(out, in_, scale, r, w):
            P.op('act', lambda e: e.activation(out, in_, AF.Sqrt, bias=EPS, scale=scale), r, w)
            P.op('dve', lambda e: e.reciprocal(out, out), w, w)

        def memset(ap, val, w, eng='dve'):
            P.op(eng, lambda e: e.memset(ap, val), (), w)

        def dma(out, in_, r, w, q='sp'):
            P.dma(q, lambda e: e.dma_start(out=out, in_=in_), r, w)

        cst = sb("cst", [128, NCONST])
        ident_f = cst[:, 0:128]; tri = cst[:, 128:256]; caus = cst[:, 256:384]; ones_f = cst[:, 384:512]
        iota = cst[:, 512:513]; caus_s = cst[:, 513:521]
        ident_b = sb("ident_b", [128, 128], BF16)
        nmix = sb("nmix", [128, KC]); nffn = sb("nffn", [128, KC])
        scw = sb("scw", [128, XT, 4]); scb = sb("scb", [128, XT])
        dtb = sb("dtb", [128, NH]); a_bc = sb("a_bc", [128, NH]); sd_bc = sb("sd_bc", [128, NH])
        snorm = sb("snorm", [128, KCM])
        cw = sb("cw", [128, CT, 31]); cb = sb("cb", [128, CT]); lng = sb("lng", [128, CT]); lnb = sb("lnb", [128, CT])
        hT = sb("hT", [128, KC, TM], BF16)
        NSLOT = 4
        wsl = [sb("wsl%d" % i, [128, 8, 512], BF16) for i in range(NSLOT)]
        mT = sb("mT", [128, KC, TM], BF16)
        ssmT = [sb("ssmT%d" % i, [128, DM]) for i in range(2)]
        ssmTb = sb("ssmTb", [128, DM], BF16)
        xtail = [sb("xtail%d" % i, [128, XT, 3]) for i in range(2)]
        utail = [sb("utail%d" % i, [128, CT, 30]) for i in range(2)]
        Kc = sb("Kc", [128, 4, SEQ], BF16)
        Vc = sb("Vc", [128, SEQ // 128, 512], BF16)
        kmT = sb("kmT", [128, 4, max(NBP, 1)])
        khalf = sb("khalf", [128, 4])
        idx = sb("idx", [128, NPG], I32)
        st1 = sb("st1", [128, 8])

        dma(cst[:, :], consts[:, :], [], ['cst'])
        cp(ident_b[:, :], ident_f, ['cst'], ['ident_b'])
        with ExitStack() as s0:
            pti = sb("pti", [128, NPG], I32, s0); ptf = sb("ptf", [128, NPG], F32, s0)
            dma(pti[:, :], pt[0:1, :].broadcast_to([128, NPG]), [], ['pti'])
            cp(ptf[:, :], pti[:, :], ['pti'], ['ptf'])
            ts(ptf[:, :], ptf[:, :], 128.0, iota, ALU.mult, ALU.add, ['ptf', 'cst'], ['ptf'])
            cp(idx[:, :], ptf[:, :], ['ptf'], ['idx'])
            P.barrier()

        for l in range(L):
            for nm in ('in', 'ssd', 'conv', 'att', 'o', 'up', 'down'):
                wf = wsrc[nm][l]
                K_, N_ = wf.shape
                nb = (N_ + 8191) // 8192
                cw_ = (N_ + nb - 1) // nb
                for r0 in range(0, K_, 1024):
                    r1 = min(K_, r0 + 1024)
                    for c0_ in range(0, N_, cw_):
                        c1_ = min(N_, c0_ + cw_)
                        dma(Wb[(nm, l)][r0:r1, c0_:c1_], wf[r0:r1, c0_:c1_], [], [('wb', nm, l)], q='pool')

        def wslot():
            s = P.nslot % NSLOT
            P.nslot += 1
            return s

        def linear(W, k0, kcn, c0, ncols, srcT, src_id, segs, consume):
            slots = []
            nsl = (kcn + 7) // 8
            for si in range(nsl):
                kk = min(8, kcn - si * 8)
                s = wslot()
                r0 = (k0 + si * 8) * 128
                src = Wb[W][r0:r0 + kk * 128, c0:c0 + ncols].rearrange("(kc p) n -> p kc n", p=128)
                dma(wsl[s][:, 0:kk, 0:ncols], src, [('wb',) + W], [('ws', s)], q='sp')
                slots.append((s, kk))
            for sg in segs:
                bank = P.bank()
                n = sg['n']
                first = True
                for si, (s, kk) in enumerate(slots):
                    for j in range(kk):
                        last = (si == nsl - 1 and j == kk - 1)
                        mm(ps[:n, bank, 0:ncols], srcT[:, si * 8 + j, sg['c0']:sg['c0'] + n], wsl[s][:, j, 0:ncols],
                           first, last, [('ws', s), src_id], [('ps', bank)], signal=(j == kk - 1))
                        first = False
                consume(sg, bank)

        def transpose_to(dstT, dst_id, src_b, src_id, n, col0, nct, gain=None, gain_id=None):
            for c8 in range(0, nct, 8):
                m = min(8, nct - c8)
                bb = P.bbank()
                for j in range(m):
                    tr(pb[:, bb, j * 128:j * 128 + n], src_b[:n, (c8 + j) * 128:(c8 + j + 1) * 128], ident_b[:n, :n],
                       [src_id, 'ident_b'], [('pb', bb)], signal=(j == m - 1))
                pv = pb[:, bb, 0:m * 128].rearrange("p (j t) -> p j t", t=128)[:, :, 0:n]
                if gain is None:
                    cp(dstT[:, c8:c8 + m, col0:col0 + n], pv, [('pb', bb)], [dst_id], eng='act')
                else:
                    tt(dstT[:, c8:c8 + m, col0:col0 + n], pv, bc(gain[:, c8:c8 + m], 2, n), ALU.mult,
                       [('pb', bb), gain_id], [dst_id])

        def rms_to_T(sg, xap, xid, gain, gain_id, stack_hb):
            n = sg['n']
            hb = stack_hb
            act(hb[:n, :], xap, AF.Square, [xid], ['hb', 'st1'], accum_out=st1[:n, 0:1])
            rsqrt(st1[:n, 1:2], st1[:n, 0:1], 1.0 / D, ['st1'], ['st1'])
            ts(hb[:n, :], xap, st1[:n, 1:2], None, ALU.mult, None, [xid, 'st1'], ['hb'])
            transpose_to(hT, 'hT', hb, 'hb', n, sg['c0'], KC, gain, gain_id)

        for l in range(L):
            dma(nmix[:, :], p_nmix[l], [], ['nmix']); dma(nffn[:, :], p_nffn[l], [], ['nffn'])
            dma(scw[:, :, :], p_scw[l], [], ['scw']); dma(scb[:, :], p_scb[l], [], ['scb'])
            dma(dtb[:, :], p_dtb[l].broadcast_to([128, NH]), [], ['dtb'])
            dma(a_bc[:, :], p_alog[l].broadcast_to([128, NH]), [], ['a_bc'])
            dma(sd_bc[:, :], p_sd[l].broadcast_to([128, NH]), [], ['sd_bc'])
            dma(snorm[:, :], p_snorm[l], [], ['snorm'])
            dma(cw[:, :, :], p_cw[l], [], ['cw']); dma(cb[:, :], p_cb[l], [], ['cb'])
            dma(lng[:, :], p_lng[l], [], ['lng']); dma(lnb[:, :], p_lnb[l], [], ['lnb'])
            act(a_bc[:, :], a_bc[:, :], AF.Exp, ['a_bc'], ['a_bc'])
            ts(a_bc[:, :], a_bc[:, :], -1.0, None, ALU.mult, None, ['a_bc'], ['a_bc'])
            memset(ssmT[0][:, :], 0.0, [('ssmT', 0)]); memset(xtail[0][:, :, :], 0.0, [('xtail', 0)])
            memset(utail[0][:, :, :], 0.0, [('utail', 0)])
            dma(ssmT[1][:, :], i_ssmT[l], [], [('ssmT', 1)])
            dma(xtail[1][:, :, :], i_xtail[l], [], [('xtail', 1)])
            dma(utail[1][:, :, :], i_utail[l], [], [('utail', 1)])
            dma(o_conv_s[l, 0:22, :], i_convrows[l, 8:30, :], [], [])
            P.barrier()

            xin_p = xp if l == 0 else xres_p
            xin_s = xs if l == 0 else xres_s
            W_in = ('in', l)

            for ti in range(NT):
                segs = []
                for j in range(TSEG):
                    pos0 = ti * TP + j * 128
                    segs.append(dict(i=j, kind='p', n=128, c0=j * 128, pos0=pos0, st=0,
                                     last=(pos0 + 128 == SEQ), ex=3 + j * 128, eu=30 + j * 128))
                if ti == NT - 1:
                    segs.append(dict(i=TSEG, kind='s', n=8, c0=TP, pos0=c.PAST, st=1, last=True, ex=TP + 6, eu=TP + 60))
                parts = [dict(st=0, c0=0, n=TP, xb=0, ub=0)]
                if ti == NT - 1:
                    parts.append(dict(st=1, c0=TP, n=8, xb=TP + 3, ub=TP + 30))
                TT = TP + (8 if ti == NT - 1 else 0)

                with ExitStack() as ph:
                    hb = sb("hb", [128, D], BF16, ph)
                    x0 = [sb("x0", [128, D], F32, ph) for _ in range(2)]
                    for sg in segs:
                        n = sg['n']
                        src = (xin_p[sg['pos0']:sg['pos0'] + n, :] if sg['kind'] == 'p' else xin_s[0:8, :])
                        xk = sg['i'] % 2
                        dma(x0[xk][:n, :], src, [], [('x0', xk)])
                        rms_to_T(sg, x0[xk][:n, :], ('x0', xk), nmix, 'nmix', hb)
                    P.barrier()
                brs = ExitStack()
                brT = [sb("brT%d" % i, [128, KCM, TM], BF16, brs) for i in range(3)]

                with ExitStack() as ph:
                    xa = sb("xa", [128, XT, TM], BF16, ph)
                    dtv = sb("dtv", [128, NSEG, NH], F32, ph)
                    with ExitStack() as ph2:
                        xbt = sb("xbt", [128, 1024], F32, ph2)
                        xext = sb("xext", [128, 8, 6 + TM], F32, ph2)
                        cacc = [sb("cacc", [128, TP], F32, ph2) for _ in range(2)]
                        for c8 in range(0, XT, 8):
                            m = min(8, XT - c8)
                            for sg in segs:
                                n = sg['n']
                                for cc0 in range(0, m * 128, 512):
                                    def cons(sg_, bank, cc0=cc0):
                                        cp(xbt[:sg_['n'], cc0:cc0 + 512], ps[:sg_['n'], bank, 0:512], [('ps', bank)], ['xbt'],
                                           eng='act')
                                    linear(W_in, 0, KC, c.oxbc + c8 * 128 + cc0, 512, hT, 'hT', [sg], cons)
                                if sg['last']:
                                    o = o_sconv_p if sg['kind'] == 'p' else o_sconv_s
                                    dma(o[l, 0:3, c8 * 128:(c8 + m) * 128], xbt[n - 3:n, 0:m * 128], ['xbt'], [])
                                for c4 in range(0, m, 4):
                                    bank = P.bank()
                                    for j in range(4):
                                        tr(ps[:, bank, j * 128:j * 128 + n], xbt[:n, (c4 + j) * 128:(c4 + j + 1) * 128],
                                           ident_f[:n, :n], ['xbt', 'cst'], [('ps', bank)], signal=(j == 3))
                                    pv = ps[:, bank, :].rearrange("p (j t) -> p j t", t=128)[:, :, 0:n]
                                    cp(xext[:, c4:c4 + 4, sg['ex']:sg['ex'] + n], pv, [('ps', bank)], ['xext'], eng='act')
                            for pr in parts:
                                n, c0, si, xb = pr['n'], pr['c0'], pr['st'], pr['xb']
                                cp(xext[:, 0:m, xb:xb + 3], xtail[si][:, c8:c8 + m, :], [('xtail', si), 'xext'], ['xext'])
                                for jc in range(m):
                                    ct = c8 + jc
                                    acc = cacc[jc % 2]
                                    aid = ('cacc', jc % 2)
                                    ts(acc[:, 0:n], xext[:, jc, xb:xb + n], scw[:, ct, 0:1], scb[:, ct:ct + 1], ALU.mult, ALU.add,
                                       ['xext', 'scw', 'scb'], [aid])
                                    for j in range(1, 4):
                                        stt(acc[:, 0:n], xext[:, jc, xb + j:xb + j + n], scw[:, ct, j:j + 1], acc[:, 0:n],
                                            ALU.mult, ALU.add, ['xext', 'scw', aid], [aid])
                                    act(xa[:, ct, c0:c0 + n], acc[:, 0:n], AF.Silu, [aid], ['xa'])
                                if n >= 3:
                                    cp(xtail[si][:, c8:c8 + m, :], xext[:, 0:m, xb + n:xb + n + 3], ['xext'], [('xtail', si)])
                        P.barrier()

                    def cons_dt(sg_, bank):
                        n_, i_ = sg_['n'], sg_['i']
                        tt(dtv[:n_, i_, :], ps[:n_, bank, 0:NH], dtb[:n_, :], ALU.add, [('ps', bank), 'dtb'], [('dtv', i_)])
                        act(dtv[:n_, i_, :], dtv[:n_, i_, :], AF.Exp, [('dtv', i_)], [('dtv', i_)])
                        act(dtv[:n_, i_, :], dtv[:n_, i_, :], AF.Ln, [('dtv', i_)], [('dtv', i_)], bias=1.0)
                    linear(W_in, 0, KC, c.odt, NH, hT, 'hT', segs, cons_dt)

                    with ExitStack() as ph2:
                        x_tok = sb("x_tok", [128, DM], BF16, ph2)
                        B_tok = sb("B_tok", [128, 512], BF16, ph2)
                        dta = sb("dta", [128, NH], F32, ph2); acum = sb("acum", [128, NH], F32, ph2)
                        nacum = sb("nacum", [128, NH], F32, ph2); cdec = sb("cdec", [128, NH], F32, ph2)
                        toend = sb("toend", [128, NH], F32, ph2); eacum = sb("eacum", [128, NH], F32, ph2)
                        tmpn = sb("tmpn", [128, NH], F32, ph2)
                        cbTm = sb("cbTm", [128, 128], F32, ph2)
                        rhsall = sb("rhsall", [128, R, 128], F32, ph2)
                        tmpA = sb("tmpA", [128, HP, 128], F32, ph2)
                        WT = sb("WT", [128, R, 128], BF16, ph2)
                        yoff = sb("yoff", [128, R, 64], F32, ph2); tmpx = sb("tmpx", [128, R, 64], F32, ph2)
                        xw = sb("xw", [128, R, 64], BF16, ph2)
                        y_seg = sb("y_seg", [128, DM], F32, ph2)
                        zc = [sb("zc", [128, 512], F32, ph2) for _ in range(2)]
                        ynb = sb("ynb", [128, DM], BF16, ph2)
                        ssg = sb("ssg", [128, 8], F32, ph2)
                        for sg in segs:
                            n, c0, i_, si = sg['n'], sg['c0'], sg['i'], sg['st']
                            cs = slice(c0, c0 + n)
                            dt_ = dtv[:n, i_, :]
                            tt(dta[:n, :], dt_, a_bc[:n, :], ALU.mult, [('dtv', i_), 'a_bc'], ['dta'])
                            b1 = P.bank()
                            mm(ps[:, b1, 0:NH], ones_f[:n, :], dta[:n, :], True, True, ['cst', 'dta'], [('ps', b1)], signal=True)
                            b2 = P.bank()
                            mm(ps[:n, b2, 0:NH], tri[:n, :n], dta[:n, :], True, True, ['cst', 'dta'], [('ps', b2)], signal=True)
                            cp(acum[:n, :], ps[:n, b2, 0:NH], [('ps', b2)], ['acum'])
                            ts(nacum[:n, :], acum[:n, :], -1.0, None, ALU.mult, None, ['acum'], ['nacum'])
                            act(cdec[:, :], ps[:, b1, 0:NH], AF.Exp, [('ps', b1)], ['cdec'])
                            tt(tmpn[:n, :], ps[:n, b1, 0:NH], acum[:n, :], ALU.subtract, [('ps', b1), 'acum'], ['tmpn'])
                            act(tmpn[:n, :], tmpn[:n, :], AF.Exp, ['tmpn'], ['tmpn'])
                            tt(toend[:n, :], tmpn[:n, :], dt_, ALU.mult, ['tmpn', ('dtv', i_)], ['toend'])
                            act(eacum[:n, :], acum[:n, :], AF.Exp, ['acum'], ['eacum'])
                            for c8 in range(0, KCM + 4, 8):
                                m = min(8, KCM + 4 - c8)
                                bb = P.bbank()
                                for j in range(m):
                                    tr(pb[:n, bb, j * 128:(j + 1) * 128], xa[:, c8 + j, cs], ident_b[:, :],
                                       ['xa', 'ident_b'], [('pb', bb)], signal=(j == m - 1))
                                for j in range(m):
                                    ct = c8 + j
                                    if ct < KCM:
                                        cp(x_tok[:n, ct * 128:(ct + 1) * 128], pb[:n, bb, j * 128:(j + 1) * 128],
                                           [('pb', bb)], ['x_tok'], eng='act')
                                    else:
                                        cp(B_tok[:n, (ct - KCM) * 128:(ct - KCM + 1) * 128], pb[:n, bb, j * 128:(j + 1) * 128],
                                           [('pb', bb)], ['B_tok'], eng='act')
                            cp(ssmTb[:, :], ssmT[si][:, :], [('ssmT', si)], ['ssmTb'])
                            for g in range(4):
                                hs = slice(g * R, (g + 1) * R)
                                BT = xa[:, KCM + g, cs]; CTg = xa[:, KCM + 4 + g, cs]
                                b3 = P.bank()
                                mm(ps[:n, b3, 0:n], BT, CTg, True, True, ['xa'], [('ps', b3)], signal=True)
                                tt(cbTm[:n, :n], ps[:n, b3, 0:n], tri[:n, :n], ALU.mult, [('ps', b3), 'cst'], ['cbTm'])
                                tt(rhsall[:n, :, :n], bc(tri[:n, :n], 1, R), bc(dta[:n, hs], 2, n), ALU.mult,
                                   ['cst', 'dta'], ['rhsall'])
                                for h0 in range(0, R, HP):
                                    b4 = P.bank()
                                    pv4 = ps[:n, b4, :].rearrange("p (r t) -> p r t", t=128)[:, 0:HP, 0:n]
                                    mm(pv4, ones_f[:n, :n], rhsall[:n, h0:h0 + HP, :n], True, True, ['cst', 'rhsall'],
                                       [('ps', b4)], signal=True)
                                    hh = slice(g * R + h0, g * R + h0 + HP)
                                    tt(tmpA[:n, :, :n], pv4, bc(nacum[:n, hh], 2, n), ALU.add, [('ps', b4), 'nacum'], ['tmpA'])
                                    ts(tmpA[:n, :, :n], tmpA[:n, :, :n], 0.0, None, ALU.min, None, ['tmpA'], ['tmpA'])
                                    act(tmpA[:n, :, :n], tmpA[:n, :, :n], AF.Exp, ['tmpA'], ['tmpA'])
                                    tt(tmpA[:n, :, :n], tmpA[:n, :, :n], bc(dtv[:n, i_, hh], 2, n), ALU.mult,
                                       ['tmpA', ('dtv', i_)], ['tmpA'])
                                    tt(WT[:n, h0:h0 + HP, :n], tmpA[:n, :, :n], bc(cbTm[:n, :n], 1, HP), ALU.mult,
                                       ['tmpA', 'cbTm'], ['WT'])
                                b5 = P.bank()
                                mm(ps[:n, b5, 0:R * 64], CTg, ssmTb[:, g * R * 64:(g + 1) * R * 64], True, True,
                                   ['xa', 'ssmTb'], [('ps', b5)], signal=True)
                                b6 = P.bank()
                                for r_ in range(R):
                                    hcol = (g * R + r_) * 64
                                    mm(ps[:n, b6, r_ * 64:(r_ + 1) * 64], WT[:n, r_, :n], x_tok[:n, hcol:hcol + 64], True, True,
                                       ['WT', 'x_tok'], [('ps', b6)], signal=(r_ == R - 1))
                                v5 = ps[:n, b5, 0:R * 64].rearrange("p (r d) -> p r d", d=64)
                                v6 = ps[:n, b6, 0:R * 64].rearrange("p (r d) -> p r d", d=64)
                                xv = x_tok[:n, g * R * 64:(g + 1) * R * 64].rearrange("p (r d) -> p r d", d=64)
                                tt(yoff[:n, :, :], v5, bc(eacum[:n, hs], 2, 64), ALU.mult, [('ps', b5), 'eacum'], ['yoff'])
                                tt(yoff[:n, :, :], yoff[:n, :, :], v6, ALU.add, ['yoff', ('ps', b6)], ['yoff'])
                                tt(tmpx[:n, :, :], xv, bc(sd_bc[:n, hs], 2, 64), ALU.mult, ['x_tok', 'sd_bc'], ['tmpx'])
                                yv = y_seg[:n, g * R * 64:(g + 1) * R * 64].rearrange("p (r d) -> p r d", d=64)
                                tt(yv, yoff[:n, :, :], tmpx[:n, :, :], ALU.add, ['yoff', 'tmpx'], ['y_seg'])
                                tt(xw[:n, :, :], xv, bc(toend[:n, hs], 2, 64), ALU.mult, ['x_tok', 'toend'], ['xw'])
                                b7 = P.bank()
                                mm(ps[:, b7, 0:R * 64], B_tok[:n, g * 128:(g + 1) * 128],
                                   xw[:n, :, :].rearrange("p r d -> p (r d)"), True, True, ['B_tok', 'xw'], [('ps', b7)],
                                   signal=True)
                                sv = ssmT[si][:, g * R * 64:(g + 1) * R * 64].rearrange("p (r d) -> p r d", d=64)
                                v7 = ps[:, b7, 0:R * 64].rearrange("p (r d) -> p r d", d=64)
                                tt(sv, sv, bc(cdec[:, hs], 2, 64), ALU.mult, [('ssmT', si), 'cdec'], [('ssmT', si)])
                                tt(sv, sv, v7, ALU.add, [('ssmT', si), ('ps', b7)], [('ssmT', si)])
                            if sg['last']:
                                o = o_ssm_p if sg['kind'] == 'p' else o_ssm_s
                                dma(o[l], ssmT[si][:, :], [('ssmT', si)], [])
                            for ci, cc0 in enumerate(range(0, DM, 512)):
                                def cons_z(sg_, bank, cc0=cc0, zt=zc[ci % 2], zid=('zc', ci % 2)):
                                    n_ = sg_['n']
                                    act(zt[:n_, :], ps[:n_, bank, 0:512], AF.Silu, [('ps', bank)], [zid])
                                    tt(y_seg[:n_, cc0:cc0 + 512], y_seg[:n_, cc0:cc0 + 512], zt[:n_, :], ALU.mult,
                                       ['y_seg', zid], ['y_seg'])
                                linear(W_in, 0, KC, c.oz + cc0, 512, hT, 'hT', [sg], cons_z)
                            for g in range(4):
                                act(ynb[:n, g * GS:(g + 1) * GS], y_seg[:n, g * GS:(g + 1) * GS], AF.Square, ['y_seg'],
                                    ['ynb', 'ssg'], accum_out=ssg[:n, g:g + 1])
                            rsqrt(ssg[:n, 4:8], ssg[:n, 0:4], 1.0 / GS, ['ssg'], ['ssg'])
                            tt(ynb[:n, :].rearrange("p (g d) -> p g d", g=4), y_seg[:n, :].rearrange("p (g d) -> p g d", g=4),
                               bc(ssg[:n, 4:8], 2, GS), ALU.mult, ['y_seg', 'ssg'], ['ynb'])
                            transpose_to(brT[0], ('brT', 0), ynb, 'ynb', n, c0, KCM, snorm, 'snorm')
                        P.barrier()

                with ExitStack() as ph:
                    uext = sb("uext", [128, CT, 60 + TM], F32, ph)
                    with ExitStack() as ph2:
                        ut = sb("ut", [128, CC], F32, ph2)
                        sgb = sb("sgb", [128, 512], F32, ph2)
                        for sg in segs:
                            n = sg['n']
                            for cc0 in range(0, CC, 512):
                                def cons_g(sg_, bank):
                                    act(sgb[:sg_['n'], :], ps[:sg_['n'], bank, 0:512], AF.Sigmoid, [('ps', bank)], ['sgb'])

                                def cons_v(sg_, bank, cc0=cc0):
                                    tt(ut[:sg_['n'], cc0:cc0 + 512], ps[:sg_['n'], bank, 0:512], sgb[:sg_['n'], :], ALU.mult,
                                       [('ps', bank), 'sgb'], ['ut'])
                                linear(W_in, 0, KC, c.oconv + CC + cc0, 512, hT, 'hT', [sg], cons_g)
                                linear(W_in, 0, KC, c.oconv + cc0, 512, hT, 'hT', [sg], cons_v)
                            if sg['last']:
                                if sg['kind'] == 'p':
                                    dma(o_conv_p[l, 0:30, :], ut[n - 30:n, :], ['ut'], [])
                                else:
                                    dma(o_conv_s[l, 22:30, :], ut[0:8, :], ['ut'], [])
                            for c4 in range(0, CT, 4):
                                bank = P.bank()
                                for j in range(4):
                                    tr(ps[:, bank, j * 128:j * 128 + n], ut[:n, (c4 + j) * 128:(c4 + j + 1) * 128],
                                       ident_f[:n, :n], ['ut', 'cst'], [('ps', bank)], signal=(j == 3))
                                pv = ps[:, bank, :].rearrange("p (j t) -> p j t", t=128)[:, :, 0:n]
                                cp(uext[:, c4:c4 + 4, sg['eu']:sg['eu'] + n], pv, [('ps', bank)], ['uext'], eng='act')
                        P.barrier()
                    with ExitStack() as ph2:
                        co = sb("co", [128, CT, TM], F32, ph2)
                        sqt = [sb("sqt", [128, TM], F32, ph2) for _ in range(2)]
                        mean = sb("mean", [128, TM], F32, ph2); rstd = sb("rstd", [128, TM], F32, ph2)
                        m2 = sb("m2", [128, TM], F32, ph2)
                        for pr in parts:
                            n, c0, si, ub = pr['n'], pr['c0'], pr['st'], pr['ub']
                            cs = slice(c0, c0 + n)
                            cp(uext[:, :, ub:ub + 30], utail[si][:, :, :], [('utail', si), 'uext'], ['uext'])
                            for j in range(31):
                                for ct in range(CT):
                                    src = uext[:, ct, ub + j:ub + j + n]
                                    if j == 0:
                                        ts(co[:, ct, cs], src, cw[:, ct, 0:1], cb[:, ct:ct + 1], ALU.mult, ALU.add,
                                           ['uext', 'cw', 'cb'], [('co', ct)])
                                    else:
                                        stt(co[:, ct, cs], src, cw[:, ct, j:j + 1], co[:, ct, cs], ALU.mult, ALU.add,
                                            ['uext', 'cw', ('co', ct)], [('co', ct)])
                            if n >= 30:
                                cp(utail[si][:, :, :], uext[:, :, ub + n:ub + n + 30], ['uext'], [('utail', si)])
                            b1 = P.bank(); b2 = P.bank()
                            for ct in range(CT):
                                mm(ps[:, b1, 0:n], ones_f, co[:, ct, cs], ct == 0, ct == CT - 1, ['cst', ('co', ct)], [('ps', b1)],
                                   signal=(ct == CT - 1))
                            for ct in range(CT):
                                sid = ('sqt', ct % 2)
                                act(sqt[ct % 2][:, 0:n], co[:, ct, cs], AF.Square, [('co', ct)], [sid])
                                mm(ps[:, b2, 0:n], ones_f, sqt[ct % 2][:, 0:n], ct == 0, ct == CT - 1, ['cst', sid], [('ps', b2)],
                                   signal=True)
                            allco = [('co', ct) for ct in range(CT)]
                            ts(mean[:, cs], ps[:, b1, 0:n], 1.0 / CC, None, ALU.mult, None, [('ps', b1)], ['mean'])
                            tt(m2[:, cs], mean[:, cs], mean[:, cs], ALU.mult, ['mean'], ['m2'])
                            stt(rstd[:, cs], ps[:, b2, 0:n], 1.0 / CC, m2[:, cs], ALU.mult, ALU.subtract, [('ps', b2), 'm2'], ['rstd'])
                            rsqrt(rstd[:, cs], rstd[:, cs], 1.0, ['rstd'], ['rstd'])
                            tt(co[:, :, cs], co[:, :, cs], bc(mean[:, cs], 1, CT), ALU.subtract, allco + ['mean'], allco)
                            tt(co[:, :, cs], co[:, :, cs], bc(rstd[:, cs], 1, CT), ALU.mult, allco + ['rstd'], allco)
                            tt(co[:, :, cs], co[:, :, cs], bc(lng[:, :], 2, n), ALU.mult, allco + ['lng'], allco)
                            tt(co[:, :, cs], co[:, :, cs], bc(lnb[:, :], 2, n), ALU.add, allco + ['lnb'], allco)
                            act(brT[1][:, :, cs], co[:, :, cs], AF.Silu, allco, [('brT', 1)])
                        P.barrier()

                with ExitStack() as ph:
                    qT = sb("qT", [128, AH, TM], BF16, ph)
                    biasb = sb("biasb", [128, AH, 8], F32, ph)
                    qTsf = sb("qTsf", [128, 4, RW], F32, ph)
                    Knew = sb("Knew", [128, 4, 8], BF16, ph)
                    Vnew = sb("Vnew", [128, 512], BF16, ph)
                    for sg in segs:
                        n, c0, i_, pos0 = sg['n'], sg['c0'], sg['i'], sg['pos0']
                        cs = slice(c0, c0 + n)
                        isp = sg['kind'] == 'p'
                        with ExitStack() as ph2:
                            qt_ = sb("qt_", [128, DM], F32, ph2); kt_ = sb("kt_", [128, 512], F32, ph2)
                            vt_ = sb("vt_", [128, 512], F32, ph2)
                            qTf = sb("qTf", [128, AH, 128], F32, ph2)
                            gate = sb("gate", [128, AH, 8], F32, ph2); top8 = sb("top8", [128, AH, 8], F32, ph2)
                            thr = sb("thr", [128, AH], F32, ph2)
                            t1 = sb("t1", [128, AH, 64], F32, ph2); t2 = sb("t2", [128, AH, 64], F32, ph2)
                            qrb = sb("qrb", [128, DM], BF16, ph2); krb = sb("krb", [128, 512], BF16, ph2)
                            csn = sb("csn", [128, 128], F32, ph2)
                            rp = rope_p[pos0:pos0 + n, :] if isp else rope_s[0:8, :]
                            dma(csn[:n, :], rp, [], ['csn'])
                            for cc0 in range(0, DM, 512):
                                def cons_q(sg_, bank, cc0=cc0):
                                    cp(qt_[:sg_['n'], cc0:cc0 + 512], ps[:sg_['n'], bank, 0:512], [('ps', bank)], ['qt_'], eng='act')
                                linear(W_in, 0, KC, c.oq + cc0, 512, hT, 'hT', [sg], cons_q)

                            def cons_k(sg_, bank):
                                cp(kt_[:sg_['n'], :], ps[:sg_['n'], bank, 0:512], [('ps', bank)], ['kt_'], eng='act')

                            def cons_v2(sg_, bank):
                                cp(vt_[:sg_['n'], :], ps[:sg_['n'], bank, 0:512], [('ps', bank)], ['vt_'], eng='act')
                            linear(W_in, 0, KC, c.ok, 512, hT, 'hT', [sg], cons_k)
                            linear(W_in, 0, KC, c.ov, 512, hT, 'hT', [sg], cons_v2)

                            def rope(src, nh, sid):
                                sv = src.rearrange("p (h two d) -> p h two d", two=2, d=64)
                                x1, x2 = sv[:, :, 0, :], sv[:, :, 1, :]
                                cosb = bc(csn[:n, 0:64], 1, nh); sinb = bc(csn[:n, 64:128], 1, nh)
                                a1, a2 = t1[:n, 0:nh, :], t2[:n, 0:nh, :]
                                tt(a1, x1, cosb, ALU.mult, [sid, 'csn'], ['t1'])
                                tt(a2, x2, sinb, ALU.mult, [sid, 'csn'], ['t2'])
                                tt(a1, a1, a2, ALU.subtract, ['t1', 't2'], ['t1'])
                                tt(a2, x1, sinb, ALU.mult, [sid, 'csn'], ['t2'])
                                tt(x2, x2, cosb, ALU.mult, [sid, 'csn'], [sid])
                                tt(x2, x2, a2, ALU.add, [sid, 't2'], [sid])
                                cp(x1, a1, ['t1', sid], [sid])
                            rope(qt_[:n, :], AH, 'qt_')
                            rope(kt_[:n, :], 4, 'kt_')
                            qr, kr = qt_, kt_
                            ko = k_p if isp else k_s
                            vo = v_p if isp else v_s
                            r0 = pos0 if isp else 0
                            dma(ko[l, r0:r0 + n, :], kr[:n, :], ['kt_'], [])
                            dma(vo[l, r0:r0 + n, :], vt_[:n, :], ['vt_'], [])
                            cp(qrb[:n, :], qr[:n, :], ['qt_'], ['qrb'])
                            cp(krb[:n, :], kr[:n, :], ['kt_'], ['krb'])
                            if isp:
                                cp(Vc[:n, pos0 // 128, :], vt_[:n, :], ['vt_'], ['Vc'])
                            else:
                                cp(Vnew[:n, :], vt_[:n, :], ['vt_'], ['Vnew'])
                            transpose_to(qT, 'qT', qrb, 'qrb', n, c0, AH)
                            bb = P.bbank()
                            for j in range(4):
                                tr(pb[:, bb, j * 128:j * 128 + n], krb[:n, j * 128:(j + 1) * 128], ident_b[:n, :n],
                                   ['krb', 'ident_b'], [('pb', bb)], signal=(j == 3))
                            pv = pb[:, bb, 0:512].rearrange("p (j t) -> p j t", t=128)[:, :, 0:n]
                            if isp:
                                cp(Kc[:, :, pos0:pos0 + n], pv, [('pb', bb)], ['Kc'], eng='act')
                            else:
                                cp(Knew[:, :, 0:n], pv, [('pb', bb)], ['Knew'], eng='act')
                            for c4 in range(0, AH, 4):
                                bank = P.bank()
                                for j in range(4):
                                    tr(ps[:, bank, j * 128:j * 128 + n], qr[:n, (c4 + j) * 128:(c4 + j + 1) * 128],
                                       ident_f[:n, :n], ['qt_', 'cst'], [('ps', bank)], signal=(j == 3))
                                pv = ps[:, bank, :].rearrange("p (j t) -> p j t", t=128)[:, :, 0:n]
                                cp(qTf[:, c4:c4 + 4, 0:n], pv, [('ps', bank)], ['qTf'], eng='act')
                            if isp:
                                bank = P.bank()
                                for kv in range(4):
                                    mm(ps[:, bank, kv:kv + 1], kr[:n, kv * 128:(kv + 1) * 128], ones_f[:n, 0:1], True, True,
                                       ['kt_', 'cst'], [('ps', bank)], signal=(kv == 3))
                                if (pos0 // 128) % 2 == 0:
                                    cp(khalf[:, :], ps[:, bank, 0:4], [('ps', bank)], ['khalf'])
                                else:
                                    tt(khalf[:, :], khalf[:, :], ps[:, bank, 0:4], ALU.add, ['khalf', ('ps', bank)], ['khalf'])
                                    ts(kmT[:, :, pos0 // 256], khalf[:, :], 1.0 / 256, None, ALU.mult, None, ['khalf'], ['kmT'])
                            b = pos0 // 256
                            if isp:
                                if b > 0:
                                    memset(gate[:n, :, :], -1e30, ['gate'])
                                    bank = P.bank()
                                    for h in range(AH):
                                        mm(ps[:n, bank, h * 8:h * 8 + b], qTf[:, h, 0:n], kmT[:, h // AR, 0:b], True, True,
                                           ['qTf', 'kmT'], [('ps', bank)], signal=(h == AH - 1))
                                    pvg = ps[:n, bank, 0:AH * 8].rearrange("p (h e) -> p h e", e=8)[:, :, 0:b]
                                    cp(gate[:n, :, 0:b], pvg, [('ps', bank)], ['gate'])
                                    for h in range(AH):
                                        P.op('dve', (lambda o_, i__: (lambda e: e.max(o_, i__)))(top8[:n, h, :], gate[:n, h, :]),
                                             ['gate'], ['top8'])
                                    ts(thr[:n, :], top8[:n, :, 2], -1e29, None, ALU.max, None, ['top8'], ['thr'])
                                    tt(biasb[:n, :, :], gate[:n, :, :], bc(thr[:n, :], 2, 8), ALU.is_ge, ['gate', 'thr'], ['biasb'])
                                    ts(biasb[:n, :, :], biasb[:n, :, :], -1.0, -NEG, ALU.add, ALU.mult, ['biasb'], ['biasb'])

                            else:
                                for g in range(4):
                                    cp(qTsf[:, g, :].rearrange("p (r q) -> p r q", q=8), qTf[:, g * AR:(g + 1) * AR, 0:8],
                                       ['qTf'], ['qTsf'])
                            P.barrier()

                        if isp:
                            with ExitStack() as ph2:
                                b = pos0 // 256
                                kend = pos0 + n
                                nkt = kend // 128
                                sm = sb("sm", [128, SEQ], F32, ph2); pe_ = sb("pe_", [128, SEQ], F32, ph2)
                                Pn = sb("Pn", [128, SEQ], BF16, ph2)
                                PTs = sb("PTs", [128, SEQ // 128, 128], BF16, ph2)
                                mx = sb("mx", [128, 4], F32, ph2)
                                for h in range(AH):
                                    kv = h // AR
                                    for k0 in range(0, kend, 512):
                                        kw = min(512, kend - k0)
                                        bank = P.bank()
                                        mm(ps[:n, bank, 0:kw], qT[:, h, cs], Kc[:, kv, k0:k0 + kw], True, True, ['qT', 'Kc'],
                                           [('ps', bank)], signal=True)
                                        for j0 in range(0, kw, 256):
                                            jb = (k0 + j0) // 256
                                            if jb < b:
                                                ts(sm[:n, k0 + j0:k0 + j0 + 256], ps[:n, bank, j0:j0 + 256], SCALE,
                                                   biasb[:n, h, jb:jb + 1], ALU.mult, ALU.add, [('ps', bank), 'biasb'], ['sm'])
                                            else:
                                                a0 = k0 + j0
                                                if a0 < pos0:
                                                    ts(sm[:n, a0:pos0], ps[:n, bank, j0:j0 + (pos0 - a0)], SCALE, None, ALU.mult,
                                                       None, [('ps', bank)], ['sm'])
                                                jj = pos0 - k0
                                                stt(sm[:n, pos0:kend], ps[:n, bank, jj:jj + n], SCALE, caus[:n, :n], ALU.mult,
                                                    ALU.add, [('ps', bank), 'cst'], ['sm'])
                                    red(mx[:n, 0:1], sm[:n, 0:kend], ALU.max, ['sm'], ['mx'])
                                    ts(mx[:n, 1:2], mx[:n, 0:1], -1.0, None, ALU.mult, None, ['mx'], ['mx'])
                                    act(pe_[:n, 0:kend], sm[:n, 0:kend], AF.Exp, ['sm', 'mx'], ['pe_', 'mx2'],
                                        bias=mx[:n, 1:2], scale=1.0, accum_out=mx[:n, 2:3])
                                    P.op('dve', (lambda o_, i__: (lambda e: e.reciprocal(o_, i__)))(mx[:n, 3:4], mx[:n, 2:3]),
                                         ['mx2'], ['mx3'])
                                    ts(Pn[:n, 0:kend], pe_[:n, 0:kend], mx[:n, 3:4], None, ALU.mult, None, ['pe_', 'mx3'], ['Pn'])
                                    for k8 in range(0, nkt, 8):
                                        m = min(8, nkt - k8)
                                        bb = P.bbank()
                                        for j in range(m):
                                            tr(pb[:, bb, j * 128:j * 128 + n], Pn[:n, (k8 + j) * 128:(k8 + j + 1) * 128],
                                               ident_b[:n, :n], ['Pn', 'ident_b'], [('pb', bb)], signal=(j == m - 1))
                                        pv = pb[:, bb, 0:m * 128].rearrange("p (j t) -> p j t", t=128)[:, :, 0:n]
                                        cp(PTs[:, k8:k8 + m, 0:n], pv, [('pb', bb)], ['PTs'], eng='act')
                                    bo = P.bank()
                                    for kt in range(nkt):
                                        mm(ps[:, bo, 0:n], Vc[:, kt, kv * 128:(kv + 1) * 128], PTs[:, kt, 0:n], kt == 0,
                                           kt == nkt - 1, ['Vc', 'PTs'], [('ps', bo)], signal=(kt == nkt - 1))
                                    cp(brT[2][:, h, cs], ps[:, bo, 0:n], [('ps', bo)], [('brT', 2)], eng='act')
                                P.barrier()
                        else:
                            with ExitStack() as ph2:
                                ckl, cvl = ck[l], cv[l]
                                qTs = sb("qTs", [128, 4, RW], BF16, ph2)
                                kp2 = sb("kp2", [128, 2, 512], F32, ph2); vp2 = sb("vp2", [128, 2, 512], F32, ph2)
                                vb2 = sb("vb2", [128, 2, 512], BF16, ph2)
                                KTc = sb("KTc", [128, 4, 256], BF16, ph2)
                                kmS = sb("kmS", [128, 4, NBS], F32, ph2)
                                gs = sb("gs", [128, 4, NBSP], F32, ph2); tp8 = sb("tp8", [128, 4, 8], F32, ph2)
                                thS = sb("thS", [128, 4], F32, ph2); bS = sb("bS", [128, 4, NBSP], F32, ph2)
                                smS = sb("smS", [128, 4, 256], F32, ph2); Pf = sb("Pf", [128, 4, 256], F32, ph2)
                                PT2 = sb("PT2", [128, 8, RW], BF16, ph2)
                                Oa = sb("Oa", [128, 4, 128], F32, ph2); On = sb("On", [128, 4, 128], BF16, ph2)
                                fm = sb("fm", [128, 4], F32, ph2); fl = sb("fl", [128, 4], F32, ph2)
                                fc = sb("fc", [128, 4], F32, ph2); fn_ = sb("fn_", [128, 4], F32, ph2)
                                fa = sb("fa", [128, 4], F32, ph2); frs = sb("frs", [128, 4], F32, ph2)
                                for g in range(4):
                                    cp(qTs[:, g, :].rearrange("p (r q) -> p r q", q=8), qT[:, g * AR:(g + 1) * AR, cs],
                                       ['qT'], ['qTs'])

                                def gather(dst, dst_id, src, j):
                                    P.dma('pool', lambda e: e.indirect_dma_start(
                                        out=dst, out_offset=None, in_=src[:, :],
                                        in_offset=bass.IndirectOffsetOnAxis(ap=idx[:, j:j + 1], axis=0)), ['idx'], [dst_id])
                                bk = P.bank()
                                for blk in range(NBS):
                                    for pg in range(2):
                                        gather(kp2[:, pg, :], ('kp2', pg), ckl, 2 * blk + pg)
                                    for kv in range(4):
                                        col = kv * NBS + blk
                                        for pg in range(2):
                                            mm(ps[:, bk, col:col + 1], kp2[:, pg, kv * 128:(kv + 1) * 128], ones_f[:, 0:1],
                                               pg == 0, pg == 1, [('kp2', pg), 'cst'], [('ps', bk)],
                                               signal=(kv == 3 and pg == 1))
                                ts(kmS[:, :, :], ps[:, bk, 0:4 * NBS].rearrange("p (k b) -> p k b", b=NBS), 1.0 / 256, None,
                                   ALU.mult, None, [('ps', bk)], ['kmS'])
                                memset(gs[:RW, :, :], -1e30, ['gs'])
                                bg = P.bank()
                                for g in range(4):
                                    mm(ps[:RW, bg, g * NBS:(g + 1) * NBS], qTsf[:, g, :], kmS[:, g, :], True, True,
                                       ['qTsf', 'kmS'], [('ps', bg)], signal=(g == 3))
                                cp(gs[:RW, :, 0:NBS], ps[:RW, bg, 0:4 * NBS].rearrange("p (k b) -> p k b", b=NBS),
                                   [('ps', bg)], ['gs'])
                                for g in range(4):
                                    P.op('dve', (lambda o_, i__: (lambda e: e.max(o_, i__)))(tp8[:RW, g, :], gs[:RW, g, :]),
                                         ['gs'], ['tp8'])
                                ts(thS[:RW, :], tp8[:RW, :, 2], -1e29, None, ALU.max, None, ['tp8'], ['thS'])
                                tt(bS[:RW, :, :], gs[:RW, :, :], bc(thS[:RW, :], 2, NBSP), ALU.is_ge, ['gs', 'thS'], ['bS'])
                                ts(bS[:RW, :, :], bS[:RW, :, :], -1.0, -NEG, ALU.add, ALU.mult, ['bS'], ['bS'])
                                memset(fm[:RW, :], -1e30, ['fm']); memset(fl[:RW, :], 0.0, ['fl'])
                                memset(Oa[:RW, :, :], 0.0, ['Oa'])

                                def online(width, nkt_, bias_ap, bias_ids, score_mm, pv_mm):
                                    banks = score_mm()
                                    for (bnk, g0, ng) in banks:
                                        pvs = ps[:RW, bnk, 0:ng * width].rearrange("p (g k) -> p g k", k=width)
                                        stt(smS[:RW, g0:g0 + ng, 0:width], pvs, SCALE, bias_ap(g0, ng), ALU.mult, ALU.add,
                                            [('ps', bnk)] + bias_ids, ['smS'])
                                    red(fc[:RW, :], smS[:RW, :, 0:width], ALU.max, ['smS'], ['fc'])
                                    tt(fn_[:RW, :], fm[:RW, :], fc[:RW, :], ALU.max, ['fm', 'fc'], ['fn_'])
                                    tt(fa[:RW, :], fm[:RW, :], fn_[:RW, :], ALU.subtract, ['fm', 'fn_'], ['fa'])
                                    act(fa[:RW, :], fa[:RW, :], AF.Exp, ['fa'], ['fa'])
                                    cp(fm[:RW, :], fn_[:RW, :], ['fn_'], ['fm'])
                                    tt(smS[:RW, :, 0:width], smS[:RW, :, 0:width], bc(fn_[:RW, :], 2, width), ALU.subtract,
                                       ['smS', 'fn_'], ['smS'])
                                    act(Pf[:RW, :, 0:width], smS[:RW, :, 0:width], AF.Exp, ['smS'], ['Pf'])
                                    red(frs[:RW, :], Pf[:RW, :, 0:width], ALU.add, ['Pf'], ['frs'])
                                    tt(fl[:RW, :], fl[:RW, :], fa[:RW, :], ALU.mult, ['fl', 'fa'], ['fl'])
                                    tt(fl[:RW, :], fl[:RW, :], frs[:RW, :], ALU.add, ['fl', 'frs'], ['fl'])
                                    kw_ = min(width, 128)
                                    bt = P.bank()
                                    for g in range(4):
                                        for kt in range(nkt_):
                                            col = (g * nkt_ + kt) * RW
                                            tr(ps[:kw_, bt, col:col + RW], Pf[:RW, g, kt * 128:kt * 128 + kw_], ident_f[:RW, :RW],
                                               ['Pf', 'cst'], [('ps', bt)], signal=(g == 3 and kt == nkt_ - 1))
                                    cp(PT2[:kw_, 0:4 * nkt_, :], ps[:kw_, bt, 0:4 * nkt_ * RW].rearrange("p (a r) -> p a r", r=RW),
                                       [('ps', bt)], ['PT2'], eng='act')
                                    bo = P.bank()
                                    pv_mm(bo, kw_)
                                    tt(Oa[:RW, :, :], Oa[:RW, :, :], bc(fa[:RW, :], 2, 128), ALU.mult, ['Oa', 'fa'], ['Oa'])
                                    tt(Oa[:RW, :, :], Oa[:RW, :, :], ps[:RW, bo, :].rearrange("p (g d) -> p g d", d=128), ALU.add,
                                       ['Oa', ('ps', bo)], ['Oa'])

                                for blk in range(NBS):
                                    for pg in range(2):
                                        gather(kp2[:, pg, :], ('kp2', pg), ckl, 2 * blk + pg)
                                        gather(vp2[:, pg, :], ('vp2', pg), cvl, 2 * blk + pg)
                                    cp(vb2[:, :, :], vp2[:, :, :], [('vp2', 0), ('vp2', 1)], ['vb2'], eng='act')
                                    for pg in range(2):
                                        bank = P.bank()
                                        for kv in range(4):
                                            tr(ps[:, bank, kv * 128:(kv + 1) * 128], kp2[:, pg, kv * 128:(kv + 1) * 128], ident_f,
                                               [('kp2', pg), 'cst'], [('ps', bank)], signal=(kv == 3))
                                        cp(KTc[:, :, pg * 128:(pg + 1) * 128], ps[:, bank, :].rearrange("p (k t) -> p k t", t=128),
                                           [('ps', bank)], ['KTc'], eng='act')

                                    def score_mm():
                                        out = []
                                        for g0 in (0, 2):
                                            bnk = P.bank()
                                            for gg in range(2):
                                                mm(ps[:RW, bnk, gg * 256:(gg + 1) * 256], qTs[:, g0 + gg, :], KTc[:, g0 + gg, :],
                                                   True, True, ['qTs', 'KTc'], [('ps', bnk)], signal=(gg == 1))
                                            out.append((bnk, g0, 2))
                                        return out

                                    def pv_mm(bo, kw_):
                                        for g in range(4):
                                            for kt in range(2):
                                                mm(ps[:RW, bo, g * 128:(g + 1) * 128], PT2[:, g * 2 + kt, :],
                                                   vb2[:, kt, g * 128:(g + 1) * 128], kt == 0, kt == 1, ['PT2', 'vb2'],
                                                   [('ps', bo)], signal=(g == 3 and kt == 1))
                                    online(256, 2, (lambda g0, ng, blk=blk: bc(bS[:RW, g0:g0 + ng, blk], 2, 256)), ['bS'],
                                           score_mm, pv_mm)

                                def score_mm2():
                                    bnk = P.bank()
                                    for g in range(4):
                                        mm(ps[:RW, bnk, g * 8:(g + 1) * 8], qTs[:, g, :], Knew[:, g, :], True, True,
                                           ['qTs', 'Knew'], [('ps', bnk)], signal=(g == 3))
                                    return [(bnk, 0, 4)]

                                def pv_mm2(bo, kw_):
                                    for g in range(4):
                                        mm(ps[:RW, bo, g * 128:(g + 1) * 128], PT2[:8, g, :], Vnew[:8, g * 128:(g + 1) * 128],
                                           True, True, ['PT2', 'Vnew'], [('ps', bo)], signal=(g == 3))
                                online(8, 1, (lambda g0, ng: bc(caus_s[:RW, 0:8], 1, ng)), ['cst'], score_mm2, pv_mm2)
                                P.op('dve', (lambda o_, i__: (lambda e: e.reciprocal(o_, i__)))(frs[:RW, :], fl[:RW, :]),
                                     ['fl'], ['frs'])
                                tt(On[:RW, :, :], Oa[:RW, :, :], bc(frs[:RW, :], 2, 128), ALU.mult, ['Oa', 'frs'], ['On'])
                                bb = P.bbank()
                                for g in range(4):
                                    tr(pb[:, bb, g * RW:(g + 1) * RW], On[:RW, g, :], ident_b[:RW, :RW], ['On', 'ident_b'],
                                       [('pb', bb)], signal=(g == 3))
                                for g in range(4):
                                    cp(brT[2][:, g * AR:(g + 1) * AR, cs], pb[:, bb, g * RW:(g + 1) * RW].rearrange(
                                        "p (r q) -> p r q", q=8), [('pb', bb)], [('brT', 2)], eng='act')
                                P.barrier()

                with ExitStack() as ph:
                    gsb = sb("gsb", [128, NSEG, 512], F32, ph); macc = sb("macc", [128, NSEG, 512], F32, ph)
                    tmpm = sb("tmpm", [128, NSEG, 512], F32, ph); mb = sb("mb", [128, NSEG, 512], BF16, ph)
                    wouts = [('ssd', l), ('conv', l), ('att', l)]
                    for cc0 in range(0, D, 512):
                        for bi in range(3):
                            def cons_gate(sg_, bank):
                                act(gsb[:sg_['n'], sg_['i'], :], ps[:sg_['n'], bank, 0:512], AF.Sigmoid, [('ps', bank)],
                                    [('gsb', sg_['i'])])

                            def cons_y(sg_, bank, bi=bi):
                                n_, i_ = sg_['n'], sg_['i']
                                if bi == 0:
                                    tt(macc[:n_, i_, :], ps[:n_, bank, 0:512], gsb[:n_, i_, :], ALU.mult,
                                       [('ps', bank), ('gsb', i_)], [('macc', i_)])
                                else:
                                    tt(tmpm[:n_, i_, :], ps[:n_, bank, 0:512], gsb[:n_, i_, :], ALU.mult,
                                       [('ps', bank), ('gsb', i_)], [('tmpm', i_)])
                                    tt(macc[:n_, i_, :], macc[:n_, i_, :], tmpm[:n_, i_, :], ALU.add,
                                       [('macc', i_), ('tmpm', i_)], [('macc', i_)])
                            linear(W_in, 0, KC, c.og + bi * D + cc0, 512, hT, 'hT', segs, cons_gate)
                            linear(wouts[bi], 0, KCM, cc0, 512, brT[bi], ('brT', bi), segs, cons_y)
                        for sg in segs:
                            n, i_ = sg['n'], sg['i']
                            cp(mb[:n, i_, :], macc[:n, i_, :], [('macc', i_)], [('mb', i_)])
                            bb = P.bbank()
                            for j in range(4):
                                tr(pb[:, bb, j * 128:j * 128 + n], mb[:n, i_, j * 128:(j + 1) * 128], ident_b[:n, :n],
                                   [('mb', i_), 'ident_b'], [('pb', bb)], signal=(j == 3))
                            pv = pb[:, bb, 0:512].rearrange("p (j t) -> p j t", t=128)[:, :, 0:n]
                            cp(mT[:, cc0 // 128:cc0 // 128 + 4, sg['c0']:sg['c0'] + n], pv, [('pb', bb)], ['mT'], eng='act')
                    P.barrier()
                brs.close()

                with ExitStack() as ph:
                    xt = sb("xt", [128, NSEG, D], F32, ph)
                    for sg in segs:
                        n = sg['n']
                        src = (xin_p[sg['pos0']:sg['pos0'] + n, :] if sg['kind'] == 'p' else xin_s[0:8, :])
                        dma(xt[:n, sg['i'], :], src, [], [('xt', sg['i'])])
                    for cc0 in range(0, D, 512):
                        def cons_o(sg_, bank, cc0=cc0):
                            n_, i_ = sg_['n'], sg_['i']
                            tt(xt[:n_, i_, cc0:cc0 + 512], xt[:n_, i_, cc0:cc0 + 512], ps[:n_, bank, 0:512], ALU.add,
                               [('xt', i_), ('ps', bank)], [('xt', i_)])
                        linear(('o', l), 0, KC, cc0, 512, mT, 'mT', segs, cons_o)

                    hb = sb("hb", [128, D], BF16, ph)
                    r1 = sb("r1", [128, NSEG, 512], F32, ph); fb = sb("fb", [128, NSEG, 512], BF16, ph)
                    for sg in segs:
                        rms_to_T(sg, xt[:sg['n'], sg['i'], :], ('xt', sg['i']), nffn, 'nffn', hb)
                    for q in range(4):
                        for cc0 in range(0, D, 512):
                            def cons_up(sg_, bank, cc0=cc0):
                                n_, i_ = sg_['n'], sg_['i']
                                act(r1[:n_, i_, :], ps[:n_, bank, 0:512], AF.Relu, [('ps', bank)], [('r1', i_)])
                                tt(fb[:n_, i_, :], r1[:n_, i_, :], r1[:n_, i_, :], ALU.mult, [('r1', i_)], [('fb', i_)])
                                bb = P.bbank()
                                for j in range(4):
                                    tr(pb[:, bb, j * 128:j * 128 + n_], fb[:n_, i_, j * 128:(j + 1) * 128], ident_b[:n_, :n_],
                                       [('fb', i_), 'ident_b'], [('pb', bb)], signal=(j == 3))
                                pv = pb[:, bb, 0:512].rearrange("p (j t) -> p j t", t=128)[:, :, 0:n_]
                                cp(mT[:, cc0 // 128:cc0 // 128 + 4, sg_['c0']:sg_['c0'] + n_], pv, [('pb', bb)], ['mT'], eng='act')
                            linear(('up', l), 0, KC, q * D + cc0, 512, hT, 'hT', segs, cons_up)
                        for cc0 in range(0, D, 512):
                            def cons_dn(sg_, bank, cc0=cc0):
                                n_, i_ = sg_['n'], sg_['i']
                                tt(xt[:n_, i_, cc0:cc0 + 512], xt[:n_, i_, cc0:cc0 + 512], ps[:n_, bank, 0:512], ALU.add,
                                   [('xt', i_), ('ps', bank)], [('xt', i_)])
                            linear(('down', l), q * KC, KC, cc0, 512, mT, 'mT', segs, cons_dn)
                    for sg in segs:
                        n, i_ = sg['n'], sg['i']
                        isp = sg['kind'] == 'p'
                        if l < L - 1:
                            dst = xres_p[sg['pos0']:sg['pos0'] + n, :] if isp else xres_s[0:8, :]
                            dma(dst, xt[:n, i_, :], [('xt', i_)], [])
                        else:
                            act(hb[:n, :], xt[:n, i_, :], AF.Square, [('xt', i_)], ['hb', 'st1'], accum_out=st1[:n, 0:1])
                            rsqrt(st1[:n, 1:2], st1[:n, 0:1], 1.0 / D, ['st1'], ['st1'])
                            ts(xt[:n, i_, :], xt[:n, i_, :], st1[:n, 1:2], None, ALU.mult, None, [('xt', i_), 'st1'], [('xt', i_)])
                            for cc0 in range(0, D, 512):
                                dma(r1[:n, 0, :], p_nfin[0:1, cc0:cc0 + 512].broadcast_to([n, 512]), [], [('r1', 0)])
                                tt(xt[:n, i_, cc0:cc0 + 512], xt[:n, i_, cc0:cc0 + 512], r1[:n, 0, :], ALU.mult,
                                   [('xt', i_), ('r1', 0)], [('xt', i_)])
                            dst = y_p[sg['pos0']:sg['pos0'] + n, :] if isp else y_s[0:8, :]
                            dma(dst, xt[:n, i_, :], [('xt', i_)], [])
                    P.barrier()
        P.barrier()
        P.emit()
    return nc


def make_consts(c):
    cst = np.zeros((128, 128 * 4 + 1 + 8), np.float32)
    i = np.arange(128)
    cst[:, 0:128] = np.eye(128, dtype=np.float32)
    cst[:, 128:256] = (i[:, None] <= i[None, :]).astype(np.float32)
    cst[:, 256:384] = np.where(i[None, :] <= i[:, None], 0.0, NEG)
    cst[:, 384:512] = 1.0
    cst[:, 512] = i
    cst[:, 513:521] = np.where(np.arange(8)[None, :] <= (i % 8)[:, None], 0.0, NEG)
    return cst


def rope_table(pos):
    half = 64
    inv = np.exp(np.float32(-math.log(10000.0)) * np.arange(half, dtype=np.float32) * np.float32(2.0) / np.float32(128))
    inv = inv.astype(np.float32)
    ang = (pos.astype(np.float32)[:, None] * inv[None, :]).astype(np.float32)
    return np.concatenate([np.cos(ang), np.sin(ang)], axis=1).astype(np.float32)


def featmajor(v, ncol_tiles):
    v = np.asarray(v)
    lead = v.shape[:-1]
    out = v.reshape(lead + (ncol_tiles, 128))
    out = np.moveaxis(out, -1, 0)
    out = np.moveaxis(out, -1, 1)
    return np.ascontiguousarray(out)


def run(inp, c, TSEG=1, ncores=8):
    L = c.L
    nc = build(c, TSEG)
    f = lambda a: np.ascontiguousarray(np.asarray(a, dtype=np.float32))
    shared = {
        "w_in": f(inp["w_in"]), "w_ssd": f(inp["w_ssd_out"]), "w_conv": f(inp["w_conv_out"]), "w_att": f(inp["w_att_out"]),
        "w_o": f(inp["w_o"]), "w_up": f(inp["w_up"]), "w_down": f(inp["w_down"]),
        "p_nmix": np.stack([featmajor(inp["norm_mix"][l], c.KC) for l in range(L)]),
        "p_nffn": np.stack([featmajor(inp["norm_ffn"][l], c.KC) for l in range(L)]),
        "p_nfin": f(inp["norm_final"]).reshape(1, c.D),
        "p_scw": np.stack([featmajor(inp["ssd_conv_w"][l], c.XT) for l in range(L)]),
        "p_scb": np.stack([featmajor(inp["ssd_conv_b"][l], c.XT) for l in range(L)]),
        "p_dtb": f(inp["ssd_dt_bias"]).reshape(L, 1, c.NH), "p_alog": f(inp["ssd_a_log"]).reshape(L, 1, c.NH),
        "p_sd": f(inp["ssd_d"]).reshape(L, 1, c.NH),
        "p_snorm": np.stack([featmajor(inp["ssd_norm"][l], c.KCM) for l in range(L)]),
        "p_cw": np.stack([featmajor(inp["conv_w"][l], c.CT) for l in range(L)]),
        "p_cb": np.stack([featmajor(inp["conv_b"][l], c.CT) for l in range(L)]),
        "p_lng": np.stack([featmajor(inp["conv_ln_g"][l], c.CT) for l in range(L)]),
        "p_lnb": np.stack([featmajor(inp["conv_ln_b"][l], c.CT) for l in range(L)]),
        "consts": make_consts(c),
        "rope_p": rope_table(np.arange(c.SEQ)), "rope_s": rope_table(c.PAST + np.arange(8)),
    }
    shared = {k: f(v) for k, v in shared.items()}
    for l in range(L):
        shared["ck%d" % l] = f(inp["cache_k"][l]).reshape(c.NPHYS * 128, 512)
        shared["cv%d" % l] = f(inp["cache_v"][l]).reshape(c.NPHYS * 128, 512)
    in_maps = []
    for core in range(ncores):
        m = dict(shared)
        m["xp"] = f(inp["x_prompt"][core % 4])
        m["xs"] = f(inp["x_sample"][core])
        m["pt"] = np.ascontiguousarray(np.asarray(inp["page_table"][core], dtype=np.int32).reshape(1, c.NPG))
        ss = np.asarray(inp["state_ssm"])[:, core]
        m["i_ssmT"] = f(ss.reshape(L, c.DM, 128).transpose(0, 2, 1))
        sc = np.asarray(inp["state_ssm_conv"])[:, core]
        m["i_xtail"] = f(np.stack([featmajor(sc[l], c.XT) for l in range(L)]))
        cvs = np.asarray(inp["state_conv"])[:, core]
        m["i_utail"] = f(np.stack([featmajor(cvs[l], c.CT) for l in range(L)]))
        m["i_convrows"] = f(cvs)
        in_maps.append(m)
    res = run_bass_kernel_spmd(nc, in_maps, core_ids=list(range(ncores)))
    R = res.results
    B = 4
    y_prompt = np.stack([R[b]["y_p"] for b in range(B)])
    y_sample = np.stack([R[b]["y_s"] for b in range(8)])
    kp = np.stack([R[b]["k_p"] for b in range(B)], axis=1).reshape(L, B, c.SEQ, 4, 128)
    vp = np.stack([R[b]["v_p"] for b in range(B)], axis=1).reshape(L, B, c.SEQ, 4, 128)
    ks = np.stack([R[b]["k_s"] for b in range(8)], axis=1).reshape(L, 8, 8, 4, 128)
    vs = np.stack([R[b]["v_s"] for b in range(8)], axis=1).reshape(L, 8, 8, 4, 128)

    def unT(a):
        return np.ascontiguousarray(a.transpose(0, 2, 1)).reshape(L, c.NH, 64, 128)
    ssm_p = np.stack([unT(R[b]["o_ssm_p"]) for b in range(B)], axis=1)
    ssm_s = np.stack([unT(R[b]["o_ssm_s"]) for b in range(8)], axis=1)
    scp = np.stack([R[b]["o_sconv_p"] for b in range(B)], axis=1)
    scs = np.stack([R[b]["o_sconv_s"] for b in range(8)], axis=1)
    cvp = np.stack([R[b]["o_conv_p"] for b in range(B)], axis=1)
    cvs_ = np.stack([R[b]["o_conv_s"] for b in range(8)], axis=1)
    outs = (y_prompt, y_sample, kp, vp, ks, vs, ssm_p, ssm_s, scp, scs, cvp, cvs_)
    return tuple(np.ascontiguousarray(o, dtype=np.float32) for o in outs)


def kernel(**inputs):
    return run(inputs, Cfg(), TSEG=2)
```

```python
import math
from contextlib import ExitStack

import numpy as np
import concourse.bass as bass
import concourse.mybir as mybir
from concourse.bass_utils import run_bass_kernel_spmd

F32 = mybir.dt.float32
BF16 = mybir.dt.bfloat16
I32 = mybir.dt.int32
ALU = mybir.AluOpType
AF = mybir.ActivationFunctionType
AX = mybir.AxisListType
EPS = 1e-6
NEG = -30000.0


class Cfg:
    def __init__(s, D=4096, SEQ=2048, PAST=16384, L=2):
        s.D = D; s.DM = D // 2; s.KC = D // 128; s.KCM = s.DM // 128
        s.NH = s.DM // 64; s.R = s.NH // 4; s.XBC = s.DM + 1024; s.XT = s.XBC // 128
        s.CC = s.DM; s.CT = s.CC // 128
        s.AH = s.DM // 128; s.AR = s.AH // 4; s.RW = s.AR * 8
        s.DFF = 4 * D; s.GS = s.DM // 4
        s.oz = 0; s.oxbc = s.DM; s.odt = s.oxbc + s.XBC; s.oconv = s.odt + s.NH
        s.oq = s.oconv + 2 * s.CC; s.ok = s.oq + s.DM; s.ov = s.ok + 512; s.og = s.ov + 512
        s.NIN = s.og + 3 * D
        s.SEQ = SEQ; s.PAST = PAST; s.L = L; s.NPG = PAST // 128
        s.NPHYS = (8 * s.NPG * 5) // 4
        s.NBP = SEQ // 256; s.NBS = PAST // 256
        s.NBSP = max(8, s.NBS)


class Prog:
    ENG = ('pe', 'act', 'dve', 'pool', 'sp')

    def __init__(self, nc, stack, nlanes=6):
        self.nc = nc
        self.sem = {}
        self.count = {}
        self.ops = {e: [] for e in self.ENG}
        self.seen = {e: {} for e in self.ENG}
        self.unsig = {e: False for e in self.ENG}
        for e in self.ENG:
            self.sem[e] = stack.enter_context(nc.semaphore('s_' + e))
            self.count[e] = 0
        self.lanes = {}
        self.rr = {}
        for q in ('sp', 'pool', 'act'):
            self.lanes[q] = []
            for i in range(nlanes):
                k = 'l_%s%d' % (q, i)
                self.sem[k] = stack.enter_context(nc.semaphore(k))
                self.count[k] = 0
                self.lanes[q].append(k)
            self.rr[q] = 0
        self.buf = {}
        self.nbank = 0
        self.nbbank = 0
        self.nslot = 0

    def _deps(self, r, w):
        deps = set()
        for b in r:
            st = self.buf.get(b)
            if st and st['w']:
                deps.add(st['w'])
        for b in w:
            st = self.buf.get(b)
            if st:
                if st['w']:
                    deps.add(st['w'])
                deps.update(st['r'])
        return deps

    def _waits(self, e, deps):
        best = {}
        for (k, v) in deps:
            if k == 'pe' and e == 'pe':
                continue
            if self.seen[e].get(k, 0) >= v:
                continue
            if best.get(k, 0) < v:
                best[k] = v
        for k, v in best.items():
            self.seen[e][k] = v
        return list(best.items())

    def _commit(self, tok, r, w):
        for b in r:
            self.buf.setdefault(b, {'w': None, 'r': []})['r'].append(tok)
        for b in w:
            self.buf[b] = {'w': tok, 'r': []}

    def op(self, e, fn, r=(), w=(), signal=True):
        assert signal or e == 'pe'
        deps = self._deps(r, w)
        waits = self._waits(e, deps)
        if signal:
            self.count[e] += 1
            tok = (e, self.count[e])
            self.unsig[e] = False
        else:
            tok = (e, self.count[e] + 1)
            self.unsig[e] = True
        self.ops[e].append((waits, fn, (e, 1) if signal else None))
        self._commit(tok, r, w)
        return tok

    def dma(self, q, fn, r=(), w=()):
        lanes = self.lanes[q]
        lk = lanes[self.rr[q] % len(lanes)]
        self.rr[q] += 1
        deps = self._deps(r, w)
        if self.count[lk] > 0:
            deps.add((lk, self.count[lk]))
        waits = self._waits(q, deps)
        self.count[lk] += 16
        tok = (lk, self.count[lk])
        self.ops[q].append((waits, fn, (lk, 16)))
        self._commit(tok, r, w)
        return tok

    def barrier(self):
        for e in self.ENG:
            assert not self.unsig[e]
        toks = set()
        for k, v in self.count.items():
            if v > 0:
                toks.add((k, v))
        for e in self.ENG:
            waits = self._waits(e, toks)
            if waits:
                self.ops[e].append((waits, None, None))
        self.buf = {}

    def bank(self):
        b = self.nbank % 6
        self.nbank += 1
        return b

    def bbank(self):
        b = self.nbbank % 2
        self.nbbank += 1
        return b

    def emit(self):
        nc = self.nc

        def run(e, h):
            for waits, fn, inc in self.ops[e]:
                for k, v in waits:
                    h.wait_ge(self.sem[k], v)
                if fn is None:
                    continue
                ins = fn(h)
                if inc is not None:
                    ins.then_inc(self.sem[inc[0]], inc[1])

        with nc.Block() as block:
            @block.tensor
            def _(h):
                run('pe', h)

            @block.scalar
            def _(h):
                run('act', h)

            @block.vector
            def _(h):
                run('dve', h)

            @block.gpsimd
            def _(h):
                run('pool', h)

            @block.sync
            def _(h):
                run('sp', h)


def bc(ap, axis, size):
    u = ap.unsqueeze(axis)
    shp = list(u.shape)
    shp[axis] = size
    return u.broadcast_to(shp)


def build(c, TSEG=1):
    nc = bass.Bass("TRN2", target_bir_lowering=False)
    D, DM, KC, KCM, NH, R, XBC, XT = c.D, c.DM, c.KC, c.KCM, c.NH, c.R, c.XBC, c.XT
    CC, CT, AH, AR, RW, DFF, GS, L, SEQ = c.CC, c.CT, c.AH, c.AR, c.RW, c.DFF, c.GS, c.L, c.SEQ
    NPG, NBP, NBS, NBSP = c.NPG, c.NBP, c.NBS, c.NBSP
    TP = 128 * TSEG
    TM = TP + 8
    NSEG = TSEG + 1
    NT = SEQ // TP
    NROWS = c.NPHYS * 128
    SCALE = 128.0 ** -0.5
    HP = min(R, 4)

    def din(name, shape, dt=F32):
        return nc.dram_tensor(name, list(shape), dt, kind="ExternalInput").ap()

    def dout(name, shape):
        return nc.dram_tensor(name, list(shape), F32, kind="ExternalOutput").ap()

    xp = din("xp", [SEQ, D]); xs = din("xs", [8, D])
    ck = [din("ck%d" % l, [NROWS, 512]) for l in range(L)]
    cv = [din("cv%d" % l, [NROWS, 512]) for l in range(L)]
    pt = din("pt", [1, NPG], I32)
    i_ssmT = din("i_ssmT", [L, 128, DM]); i_xtail = din("i_xtail", [L, 128, XT, 3])
    i_utail = din("i_utail", [L, 128, CT, 30]); i_convrows = din("i_convrows", [L, 30, CC])
    w_in = din("w_in", [L, D, c.NIN]); w_ssd = din("w_ssd", [L, DM, D]); w_conv = din("w_conv", [L, DM, D])
    w_att = din("w_att", [L, DM, D]); w_o = din("w_o", [L, D, D]); w_up = din("w_up", [L, D, DFF])
    w_down = din("w_down", [L, DFF, D])
    p_nmix = din("p_nmix", [L, 128, KC]); p_nffn = din("p_nffn", [L, 128, KC]); p_nfin = din("p_nfin", [1, D])
    p_scw = din("p_scw", [L, 128, XT, 4]); p_scb = din("p_scb", [L, 128, XT])
    p_dtb = din("p_dtb", [L, 1, NH]); p_alog = din("p_alog", [L, 1, NH]); p_sd = din("p_sd", [L, 1, NH])
    p_snorm = din("p_snorm", [L, 128, KCM])
    p_cw = din("p_cw", [L, 128, CT, 31]); p_cb = din("p_cb", [L, 128, CT])
    p_lng = din("p_lng", [L, 128, CT]); p_lnb = din("p_lnb", [L, 128, CT])
    NCONST = 128 * 4 + 1 + 8
    consts = din("consts", [128, NCONST])
    rope_p = din("rope_p", [SEQ, 128]); rope_s = din("rope_s", [8, 128])

    y_p = dout("y_p", [SEQ, D]); y_s = dout("y_s", [8, D])
    k_p = dout("k_p", [L, SEQ, 512]); v_p = dout("v_p", [L, SEQ, 512])
    k_s = dout("k_s", [L, 8, 512]); v_s = dout("v_s", [L, 8, 512])
    o_ssm_p = dout("o_ssm_p", [L, 128, DM]); o_ssm_s = dout("o_ssm_s", [L, 128, DM])
    o_sconv_p = dout("o_sconv_p", [L, 3, XBC]); o_sconv_s = dout("o_sconv_s", [L, 3, XBC])
    o_conv_p = dout("o_conv_p", [L, 30, CC]); o_conv_s = dout("o_conv_s", [L, 30, CC])
    wsrc = {'in': w_in, 'ssd': w_ssd, 'conv': w_conv, 'att': w_att, 'o': w_o, 'up': w_up, 'down': w_down}
    Wb = {}
    for l in range(L):
        for nm, wfull in wsrc.items():
            K_, N_ = wfull.shape[1], wfull.shape[2]
            Wb[(nm, l)] = nc.dram_tensor("wb_%s%d" % (nm, l), [K_, N_], BF16).ap()
    xres_p = nc.dram_tensor("xres_p", [SEQ, D], F32).ap()
    xres_s = nc.dram_tensor("xres_s", [8, D], F32).ap()

    with ExitStack() as st:
        P = Prog(nc, st)

        sbn = [0]

        def sb(name, shape, dt=F32, stack=None):
            sbn[0] += 1
            return (stack or st).enter_context(nc.sbuf_tensor("%s_%d" % (name, sbn[0]), list(shape), dt))

        ps = st.enter_context(nc.psum_tensor("ps", [128, 6, 512], F32))
        pb = st.enter_context(nc.psum_tensor("pb", [128, 2, 1024], BF16))

        def mm(out, lhsT, rhs, start, stop, r, w, signal=False):
            P.op('pe', lambda e: e.matmul(out, lhsT, rhs, start=start, stop=stop), r, w, signal)

        def tr(out, in_, ident, r, w, signal=True):
            P.op('pe', lambda e: e.transpose(out, in_, ident), r, w, signal)

        def act(out, in_, func, r, w, **kw):
            P.op('act', lambda e: e.activation(out, in_, func, **kw), r, w)

        def tt(out, in0, in1, op, r, w, eng='dve'):
            P.op(eng, lambda e: e.tensor_tensor(out, in0, in1, op), r, w)

        def ts(out, in0, s1, s2, op0, op1, r, w, eng='dve'):
            if s2 is None:
                P.op(eng, lambda e: e.tensor_scalar(out, in0, s1, None, op0), r, w)
            else:
                P.op(eng, lambda e: e.tensor_scalar(out, in0, s1, s2, op0, op1), r, w)

        def stt(out, in0, scalar, in1, op0, op1, r, w, eng='dve'):
            P.op(eng, lambda e: e.scalar_tensor_tensor(out, in0, scalar, in1, op0, op1), r, w)

        def cp(out, in_, r, w, eng='dve'):
            if eng == 'act':
                P.op('act', lambda e: e.copy(out, in_), r, w)
            else:
                P.op(eng, lambda e: e.tensor_copy(out, in_), r, w)

        def red(out, in_, op, r, w):
            P.op('dve', lambda e: e.tensor_reduce(out, in_, AX.X, op), r, w)

        def rsqrt(out, in_, scale, r, w):
            P.op('act', lambda e: e.activation(out, in_, AF.Sqrt, bias=EPS, scale=scale), r, w)
            P.op('dve', lambda e: e.reciprocal(out, out), w, w)

        def memset(ap, val, w, eng='dve'):
            P.op(eng, lambda e: e.memset(ap, val), (), w)

        def dma(out, in_, r, w, q='sp'):
            P.dma(q, lambda e: e.dma_start(out=out, in_=in_), r, w)

        cst = sb("cst", [128, NCONST])
        ident_f = cst[:, 0:128]; tri = cst[:, 128:256]; caus = cst[:, 256:384]; ones_f = cst[:, 384:512]
        iota = cst[:, 512:513]; caus_s = cst[:, 513:521]
        ident_b = sb("ident_b", [128, 128], BF16)
        nmix = sb("nmix", [128, KC]); nffn = sb("nffn", [128, KC])
        scw = sb("scw", [128, XT, 4]); scb = sb("scb", [128, XT])
        dtb = sb("dtb", [128, NH]); a_bc = sb("a_bc", [128, NH]); sd_bc = sb("sd_bc", [128, NH])
        snorm = sb("snorm", [128, KCM])
        cw = sb("cw", [128, CT, 31]); cb = sb("cb", [128, CT]); lng = sb("lng", [128, CT]); lnb = sb("lnb", [128, CT])
        hT = sb("hT", [128, KC, TM], BF16)
        NSLOT = 4
        wsl = [sb("wsl%d" % i, [128, 8, 512], BF16) for i in range(NSLOT)]
        mT = sb("mT", [128, KC, TM], BF16)
        ssmT = [sb("ssmT%d" % i, [128, DM]) for i in range(2)]
        ssmTb = sb("ssmTb", [128, DM], BF16)
        xtail = [sb("xtail%d" % i, [128, XT, 3]) for i in range(2)]
        utail = [sb("utail%d" % i, [128, CT, 30]) for i in range(2)]
        Kc = sb("Kc", [128, 4, SEQ], BF16)
        Vc = sb("Vc", [128, SEQ // 128, 512], BF16)
        kmT = sb("kmT", [128, 4, max(NBP, 1)])
        khalf = sb("khalf", [128, 4])
        idx = sb("idx", [128, NPG], I32)
        st1 = sb("st1", [128, 8])

        dma(cst[:, :], consts[:, :], [], ['cst'])
        cp(ident_b[:, :], ident_f, ['cst'], ['ident_b'])
        with ExitStack() as s0:
            pti = sb("pti", [128, NPG], I32, s0); ptf = sb("ptf", [128, NPG], F32, s0)
            dma(pti[:, :], pt[0:1, :].broadcast_to([128, NPG]), [], ['pti'])
            cp(ptf[:, :], pti[:, :], ['pti'], ['ptf'])
            ts(ptf[:, :], ptf[:, :], 128.0, iota, ALU.mult, ALU.add, ['ptf', 'cst'], ['ptf'])
            cp(idx[:, :], ptf[:, :], ['ptf'], ['idx'])
            P.barrier()

        for l in range(L):
            for nm in ('in', 'ssd', 'conv', 'att', 'o', 'up', 'down'):
                wf = wsrc[nm][l]
                K_, N_ = wf.shape
                nb = (N_ + 8191) // 8192
                cw_ = (N_ + nb - 1) // nb
                for r0 in range(0, K_, 1024):
                    r1 = min(K_, r0 + 1024)
                    for c0_ in range(0, N_, cw_):
                        c1_ = min(N_, c0_ + cw_)
                        dma(Wb[(nm, l)][r0:r1, c0_:c1_], wf[r0:r1, c0_:c1_], [], [('wb', nm, l)], q='pool')

        def wslot():
            s = P.nslot % NSLOT
            P.nslot += 1
            return s

        def linear(W, k0, kcn, c0, ncols, srcT, src_id, segs, consume):
            nsl = (kcn + 7) // 8
            banks = [P.bank() for _ in segs]
            for si in range(nsl):
                kk = min(8, kcn - si * 8)
                s = wslot()
                r0 = (k0 + si * 8) * 128
                src = Wb[W][r0:r0 + kk * 128, c0:c0 + ncols].rearrange("(kc p) n -> p kc n", p=128)
                dma(wsl[s][:, 0:kk, 0:ncols], src, [('wb',) + W], [('ws', s)], q='sp')
                for gi, sg in enumerate(segs):
                    n = sg['n']
                    for j in range(kk):
                        first = (si == 0 and j == 0)
                        last = (si == nsl - 1 and j == kk - 1)
                        mm(ps[:n, banks[gi], 0:ncols], srcT[:, si * 8 + j, sg['c0']:sg['c0'] + n], wsl[s][:, j, 0:ncols],
                           first, last, [('ws', s), src_id], [('ps', banks[gi])], signal=(j == kk - 1))
                    if si == nsl - 1:
                        consume(sg, banks[gi])

        def transpose_to(dstT, dst_id, src_b, src_id, n, col0, nct, gain=None, gain_id=None):
            for c8 in range(0, nct, 8):
                m = min(8, nct - c8)
                bb = P.bbank()
                for j in range(m):
                    tr(pb[:, bb, j * 128:j * 128 + n], src_b[:n, (c8 + j) * 128:(c8 + j + 1) * 128], ident_b[:n, :n],
                       [src_id, 'ident_b'], [('pb', bb)], signal=(j == m - 1))
                pv = pb[:, bb, 0:m * 128].rearrange("p (j t) -> p j t", t=128)[:, :, 0:n]
                if gain is None:
                    cp(dstT[:, c8:c8 + m, col0:col0 + n], pv, [('pb', bb)], [dst_id], eng='act')
                else:
                    tt(dstT[:, c8:c8 + m, col0:col0 + n], pv, bc(gain[:, c8:c8 + m], 2, n), ALU.mult,
                       [('pb', bb), gain_id], [dst_id])

        def rms_to_T(sg, xap, xid, gain, gain_id, stack_hb):
            n = sg['n']
            hb = stack_hb
            act(hb[:n, :], xap, AF.Square, [xid], ['hb', 'st1'], accum_out=st1[:n, 0:1])
            rsqrt(st1[:n, 1:2], st1[:n, 0:1], 1.0 / D, ['st1'], ['st1'])
            ts(hb[:n, :], xap, st1[:n, 1:2], None, ALU.mult, None, [xid, 'st1'], ['hb'])
            transpose_to(hT, 'hT', hb, 'hb', n, sg['c0'], KC, gain, gain_id)

        for l in range(L):
            dma(nmix[:, :], p_nmix[l], [], ['nmix']); dma(nffn[:, :], p_nffn[l], [], ['nffn'])
            dma(scw[:, :, :], p_scw[l], [], ['scw']); dma(scb[:, :], p_scb[l], [], ['scb'])
            dma(dtb[:, :], p_dtb[l].broadcast_to([128, NH]), [], ['dtb'])
            dma(a_bc[:, :], p_alog[l].broadcast_to([128, NH]), [], ['a_bc'])
            dma(sd_bc[:, :], p_sd[l].broadcast_to([128, NH]), [], ['sd_bc'])
            dma(snorm[:, :], p_snorm[l], [], ['snorm'])
            dma(cw[:, :, :], p_cw[l], [], ['cw']); dma(cb[:, :], p_cb[l], [], ['cb'])
            dma(lng[:, :], p_lng[l], [], ['lng']); dma(lnb[:, :], p_lnb[l], [], ['lnb'])
            act(a_bc[:, :], a_bc[:, :], AF.Exp, ['a_bc'], ['a_bc'])
            ts(a_bc[:, :], a_bc[:, :], -1.0, None, ALU.mult, None, ['a_bc'], ['a_bc'])
            memset(ssmT[0][:, :], 0.0, [('ssmT', 0)]); memset(xtail[0][:, :, :], 0.0, [('xtail', 0)])
            memset(utail[0][:, :, :], 0.0, [('utail', 0)])
            dma(ssmT[1][:, :], i_ssmT[l], [], [('ssmT', 1)])
            dma(xtail[1][:, :, :], i_xtail[l], [], [('xtail', 1)])
            dma(utail[1][:, :, :], i_utail[l], [], [('utail', 1)])
            dma(o_conv_s[l, 0:22, :], i_convrows[l, 8:30, :], [], [])
            P.barrier()

            xin_p = xp if l == 0 else xres_p
            xin_s = xs if l == 0 else xres_s
            W_in = ('in', l)

            for ti in range(NT):
                segs = []
                for j in range(TSEG):
                    pos0 = ti * TP + j * 128
                    segs.append(dict(i=j, kind='p', n=128, c0=j * 128, pos0=pos0, st=0,
                                     last=(pos0 + 128 == SEQ), ex=3 + j * 128, eu=30 + j * 128))
                if ti == NT - 1:
                    segs.append(dict(i=TSEG, kind='s', n=8, c0=TP, pos0=c.PAST, st=1, last=True, ex=TP + 6, eu=TP + 60))
                parts = [dict(st=0, c0=0, n=TP, xb=0, ub=0)]
                if ti == NT - 1:
                    parts.append(dict(st=1, c0=TP, n=8, xb=TP + 3, ub=TP + 30))
                TT = TP + (8 if ti == NT - 1 else 0)

                with ExitStack() as ph:
                    hb = sb("hb", [128, D], BF16, ph)
                    x0 = [sb("x0", [128, D], F32, ph) for _ in range(2)]
                    for sg in segs:
                        n = sg['n']
                        src = (xin_p[sg['pos0']:sg['pos0'] + n, :] if sg['kind'] == 'p' else xin_s[0:8, :])
                        xk = sg['i'] % 2
                        dma(x0[xk][:n, :], src, [], [('x0', xk)])
                        rms_to_T(sg, x0[xk][:n, :], ('x0', xk), nmix, 'nmix', hb)
                    P.barrier()
                brs = ExitStack()
                brT = [sb("brT%d" % i, [128, KCM, TM], BF16, brs) for i in range(3)]

                with ExitStack() as ph:
                    xa = sb("xa", [128, XT, TM], BF16, ph)
                    dtv = sb("dtv", [128, NSEG, NH], F32, ph)
                    with ExitStack() as ph2:
                        xbt = sb("xbt", [128, 1024], F32, ph2)
                        xext = sb("xext", [128, 8, 6 + TM], F32, ph2)
                        cacc = [sb("cacc", [128, TP], F32, ph2) for _ in range(2)]
                        for c8 in range(0, XT, 8):
                            m = min(8, XT - c8)
                            for sg in segs:
                                n = sg['n']
                                for cc0 in range(0, m * 128, 512):
                                    def cons(sg_, bank, cc0=cc0):
                                        cp(xbt[:sg_['n'], cc0:cc0 + 512], ps[:sg_['n'], bank, 0:512], [('ps', bank)], ['xbt'],
                                           eng='act')
                                    linear(W_in, 0, KC, c.oxbc + c8 * 128 + cc0, 512, hT, 'hT', [sg], cons)
                                if sg['last']:
                                    o = o_sconv_p if sg['kind'] == 'p' else o_sconv_s
                                    dma(o[l, 0:3, c8 * 128:(c8 + m) * 128], xbt[n - 3:n, 0:m * 128], ['xbt'], [])
                                for c4 in range(0, m, 4):
                                    bank = P.bank()
                                    for j in range(4):
                                        tr(ps[:, bank, j * 128:j * 128 + n], xbt[:n, (c4 + j) * 128:(c4 + j + 1) * 128],
                                           ident_f[:n, :n], ['xbt', 'cst'], [('ps', bank)], signal=(j == 3))
                                    pv = ps[:, bank, :].rearrange("p (j t) -> p j t", t=128)[:, :, 0:n]
                                    cp(xext[:, c4:c4 + 4, sg['ex']:sg['ex'] + n], pv, [('ps', bank)], ['xext'], eng='act')
                            for pr in parts:
                                n, c0, si, xb = pr['n'], pr['c0'], pr['st'], pr['xb']
                                cp(xext[:, 0:m, xb:xb + 3], xtail[si][:, c8:c8 + m, :], [('xtail', si), 'xext'], ['xext'])
                                for jc in range(m):
                                    ct = c8 + jc
                                    acc = cacc[jc % 2]
                                    aid = ('cacc', jc % 2)
                                    ts(acc[:, 0:n], xext[:, jc, xb:xb + n], scw[:, ct, 0:1], scb[:, ct:ct + 1], ALU.mult, ALU.add,
                                       ['xext', 'scw', 'scb'], [aid])
                                    for j in range(1, 4):
                                        stt(acc[:, 0:n], xext[:, jc, xb + j:xb + j + n], scw[:, ct, j:j + 1], acc[:, 0:n],
                                            ALU.mult, ALU.add, ['xext', 'scw', aid], [aid])
                                    act(xa[:, ct, c0:c0 + n], acc[:, 0:n], AF.Silu, [aid], ['xa'])
                                if n >= 3:
                                    cp(xtail[si][:, c8:c8 + m, :], xext[:, 0:m, xb + n:xb + n + 3], ['xext'], [('xtail', si)])
                        P.barrier()

                    def cons_dt(sg_, bank):
                        n_, i_ = sg_['n'], sg_['i']
                        tt(dtv[:n_, i_, :], ps[:n_, bank, 0:NH], dtb[:n_, :], ALU.add, [('ps', bank), 'dtb'], [('dtv', i_)])
                        act(dtv[:n_, i_, :], dtv[:n_, i_, :], AF.Exp, [('dtv', i_)], [('dtv', i_)])
                        act(dtv[:n_, i_, :], dtv[:n_, i_, :], AF.Ln, [('dtv', i_)], [('dtv', i_)], bias=1.0)
                    linear(W_in, 0, KC, c.odt, NH, hT, 'hT', segs, cons_dt)

                    with ExitStack() as ph2:
                        x_tok = sb("x_tok", [128, DM], BF16, ph2)
                        B_tok = sb("B_tok", [128, 512], BF16, ph2)
                        dta = sb("dta", [128, NH], F32, ph2); acum = sb("acum", [128, NH], F32, ph2)
                        nacum = sb("nacum", [128, NH], F32, ph2); cdec = sb("cdec", [128, NH], F32, ph2)
                        toend = sb("toend", [128, NH], F32, ph2); eacum = sb("eacum", [128, NH], F32, ph2)
                        tmpn = sb("tmpn", [128, NH], F32, ph2)
                        cbTm = sb("cbTm", [128, 128], F32, ph2)
                        rhsall = sb("rhsall", [128, R, 128], F32, ph2)
                        tmpA = sb("tmpA", [128, HP, 128], F32, ph2)
                        WT = sb("WT", [128, R, 128], BF16, ph2)
                        yoff = sb("yoff", [128, R, 64], F32, ph2); tmpx = sb("tmpx", [128, R, 64], F32, ph2)
                        xw = sb("xw", [128, R, 64], BF16, ph2)
                        y_seg = sb("y_seg", [128, DM], F32, ph2)
                        zc = [sb("zc", [128, 512], F32, ph2) for _ in range(2)]
                        ynb = sb("ynb", [128, DM], BF16, ph2)
                        ssg = sb("ssg", [128, 8], F32, ph2)
                        for sg in segs:
                            n, c0, i_, si = sg['n'], sg['c0'], sg['i'], sg['st']
                            cs = slice(c0, c0 + n)
                            dt_ = dtv[:n, i_, :]
                            tt(dta[:n, :], dt_, a_bc[:n, :], ALU.mult, [('dtv', i_), 'a_bc'], ['dta'])
                            b1 = P.bank()
                            mm(ps[:, b1, 0:NH], ones_f[:n, :], dta[:n, :], True, True, ['cst', 'dta'], [('ps', b1)], signal=True)
                            b2 = P.bank()
                            mm(ps[:n, b2, 0:NH], tri[:n, :n], dta[:n, :], True, True, ['cst', 'dta'], [('ps', b2)], signal=True)
                            cp(acum[:n, :], ps[:n, b2, 0:NH], [('ps', b2)], ['acum'])
                            ts(nacum[:n, :], acum[:n, :], -1.0, None, ALU.mult, None, ['acum'], ['nacum'])
                            act(cdec[:, :], ps[:, b1, 0:NH], AF.Exp, [('ps', b1)], ['cdec'])
                            tt(tmpn[:n, :], ps[:n, b1, 0:NH], acum[:n, :], ALU.subtract, [('ps', b1), 'acum'], ['tmpn'])
                            act(tmpn[:n, :], tmpn[:n, :], AF.Exp, ['tmpn'], ['tmpn'])
                            tt(toend[:n, :], tmpn[:n, :], dt_, ALU.mult, ['tmpn', ('dtv', i_)], ['toend'])
                            act(eacum[:n, :], acum[:n, :], AF.Exp, ['acum'], ['eacum'])
                            for c8 in range(0, KCM + 4, 8):
                                m = min(8, KCM + 4 - c8)
                                bb = P.bbank()
                                for j in range(m):
                                    tr(pb[:n, bb, j * 128:(j + 1) * 128], xa[:, c8 + j, cs], ident_b[:, :],
                                       ['xa', 'ident_b'], [('pb', bb)], signal=(j == m - 1))
                                for j in range(m):
                                    ct = c8 + j
                                    if ct < KCM:
                                        cp(x_tok[:n, ct * 128:(ct + 1) * 128], pb[:n, bb, j * 128:(j + 1) * 128],
                                           [('pb', bb)], ['x_tok'], eng='act')
                                    else:
                                        cp(B_tok[:n, (ct - KCM) * 128:(ct - KCM + 1) * 128], pb[:n, bb, j * 128:(j + 1) * 128],
                                           [('pb', bb)], ['B_tok'], eng='act')
                            cp(ssmTb[:, :], ssmT[si][:, :], [('ssmT', si)], ['ssmTb'])
                            for g in range(4):
                                hs = slice(g * R, (g + 1) * R)
                                BT = xa[:, KCM + g, cs]; CTg = xa[:, KCM + 4 + g, cs]
                                b3 = P.bank()
                                mm(ps[:n, b3, 0:n], BT, CTg, True, True, ['xa'], [('ps', b3)], signal=True)
                                tt(cbTm[:n, :n], ps[:n, b3, 0:n], tri[:n, :n], ALU.mult, [('ps', b3), 'cst'], ['cbTm'])
                                tt(rhsall[:n, :, :n], bc(tri[:n, :n], 1, R), bc(dta[:n, hs], 2, n), ALU.mult,
                                   ['cst', 'dta'], ['rhsall'])
                                for h0 in range(0, R, HP):
                                    b4 = P.bank()
                                    pv4 = ps[:n, b4, :].rearrange("p (r t) -> p r t", t=128)[:, 0:HP, 0:n]
                                    mm(pv4, ones_f[:n, :n], rhsall[:n, h0:h0 + HP, :n], True, True, ['cst', 'rhsall'],
                                       [('ps', b4)], signal=True)
                                    hh = slice(g * R + h0, g * R + h0 + HP)
                                    tt(tmpA[:n, :, :n], pv4, bc(nacum[:n, hh], 2, n), ALU.add, [('ps', b4), 'nacum'], ['tmpA'])
                                    ts(tmpA[:n, :, :n], tmpA[:n, :, :n], 0.0, None, ALU.min, None, ['tmpA'], ['tmpA'])
                                    act(tmpA[:n, :, :n], tmpA[:n, :, :n], AF.Exp, ['tmpA'], ['tmpA'])
                                    tt(tmpA[:n, :, :n], tmpA[:n, :, :n], bc(dtv[:n, i_, hh], 2, n), ALU.mult,
                                       ['tmpA', ('dtv', i_)], ['tmpA'])
                                    tt(WT[:n, h0:h0 + HP, :n], tmpA[:n, :, :n], bc(cbTm[:n, :n], 1, HP), ALU.mult,
                                       ['tmpA', 'cbTm'], ['WT'])
                                b5 = P.bank()
                                mm(ps[:n, b5, 0:R * 64], CTg, ssmTb[:, g * R * 64:(g + 1) * R * 64], True, True,
                                   ['xa', 'ssmTb'], [('ps', b5)], signal=True)
                                b6 = P.bank()
                                for r_ in range(R):
                                    hcol = (g * R + r_) * 64
                                    mm(ps[:n, b6, r_ * 64:(r_ + 1) * 64], WT[:n, r_, :n], x_tok[:n, hcol:hcol + 64], True, True,
                                       ['WT', 'x_tok'], [('ps', b6)], signal=(r_ == R - 1))
                                v5 = ps[:n, b5, 0:R * 64].rearrange("p (r d) -> p r d", d=64)
                                v6 = ps[:n, b6, 0:R * 64].rearrange("p (r d) -> p r d", d=64)
                                xv = x_tok[:n, g * R * 64:(g + 1) * R * 64].rearrange("p (r d) -> p r d", d=64)
                                tt(yoff[:n, :, :], v5, bc(eacum[:n, hs], 2, 64), ALU.mult, [('ps', b5), 'eacum'], ['yoff'])
                                tt(yoff[:n, :, :], yoff[:n, :, :], v6, ALU.add, ['yoff', ('ps', b6)], ['yoff'])
                                tt(tmpx[:n, :, :], xv, bc(sd_bc[:n, hs], 2, 64), ALU.mult, ['x_tok', 'sd_bc'], ['tmpx'])
                                yv = y_seg[:n, g * R * 64:(g + 1) * R * 64].rearrange("p (r d) -> p r d", d=64)
                                tt(yv, yoff[:n, :, :], tmpx[:n, :, :], ALU.add, ['yoff', 'tmpx'], ['y_seg'])
                                tt(xw[:n, :, :], xv, bc(toend[:n, hs], 2, 64), ALU.mult, ['x_tok', 'toend'], ['xw'])
                                b7 = P.bank()
                                mm(ps[:, b7, 0:R * 64], B_tok[:n, g * 128:(g + 1) * 128],
                                   xw[:n, :, :].rearrange("p r d -> p (r d)"), True, True, ['B_tok', 'xw'], [('ps', b7)],
                                   signal=True)
                                sv = ssmT[si][:, g * R * 64:(g + 1) * R * 64].rearrange("p (r d) -> p r d", d=64)
                                v7 = ps[:, b7, 0:R * 64].rearrange("p (r d) -> p r d", d=64)
                                tt(sv, sv, bc(cdec[:, hs], 2, 64), ALU.mult, [('ssmT', si), 'cdec'], [('ssmT', si)])
                                tt(sv, sv, v7, ALU.add, [('ssmT', si), ('ps', b7)], [('ssmT', si)])
                            if sg['last']:
                                o = o_ssm_p if sg['kind'] == 'p' else o_ssm_s
                                dma(o[l], ssmT[si][:, :], [('ssmT', si)], [])
                            for ci, cc0 in enumerate(range(0, DM, 512)):
                                def cons_z(sg_, bank, cc0=cc0, zt=zc[ci % 2], zid=('zc', ci % 2)):
                                    n_ = sg_['n']
                                    act(zt[:n_, :], ps[:n_, bank, 0:512], AF.Silu, [('ps', bank)], [zid])
                                    tt(y_seg[:n_, cc0:cc0 + 512], y_seg[:n_, cc0:cc0 + 512], zt[:n_, :], ALU.mult,
                                       ['y_seg', zid], ['y_seg'])
                                linear(W_in, 0, KC, c.oz + cc0, 512, hT, 'hT', [sg], cons_z)
                            for g in range(4):
                                act(ynb[:n, g * GS:(g + 1) * GS], y_seg[:n, g * GS:(g + 1) * GS], AF.Square, ['y_seg'],
                                    ['ynb', 'ssg'], accum_out=ssg[:n, g:g + 1])
                            rsqrt(ssg[:n, 4:8], ssg[:n, 0:4], 1.0 / GS, ['ssg'], ['ssg'])
                            tt(ynb[:n, :].rearrange("p (g d) -> p g d", g=4), y_seg[:n, :].rearrange("p (g d) -> p g d", g=4),
                               bc(ssg[:n, 4:8], 2, GS), ALU.mult, ['y_seg', 'ssg'], ['ynb'])
                            transpose_to(brT[0], ('brT', 0), ynb, 'ynb', n, c0, KCM, snorm, 'snorm')
                        P.barrier()

                with ExitStack() as ph:
                    uext = sb("uext", [128, CT, 60 + TM], F32, ph)
                    with ExitStack() as ph2:
                        ut = sb("ut", [128, CC], F32, ph2)
                        sgb = sb("sgb", [128, 512], F32, ph2)
                        for sg in segs:
                            n = sg['n']
                            for cc0 in range(0, CC, 512):
                                def cons_g(sg_, bank):
                                    act(sgb[:sg_['n'], :], ps[:sg_['n'], bank, 0:512], AF.Sigmoid, [('ps', bank)], ['sgb'])

                                def cons_v(sg_, bank, cc0=cc0):
                                    tt(ut[:sg_['n'], cc0:cc0 + 512], ps[:sg_['n'], bank, 0:512], sgb[:sg_['n'], :], ALU.mult,
                                       [('ps', bank), 'sgb'], ['ut'])
                                linear(W_in, 0, KC, c.oconv + CC + cc0, 512, hT, 'hT', [sg], cons_g)
                                linear(W_in, 0, KC, c.oconv + cc0, 512, hT, 'hT', [sg], cons_v)
                            if sg['last']:
                                if sg['kind'] == 'p':
                                    dma(o_conv_p[l, 0:30, :], ut[n - 30:n, :], ['ut'], [])
                                else:
                                    dma(o_conv_s[l, 22:30, :], ut[0:8, :], ['ut'], [])
                            for c4 in range(0, CT, 4):
                                bank = P.bank()
                                for j in range(4):
                                    tr(ps[:, bank, j * 128:j * 128 + n], ut[:n, (c4 + j) * 128:(c4 + j + 1) * 128],
                                       ident_f[:n, :n], ['ut', 'cst'], [('ps', bank)], signal=(j == 3))
                                pv = ps[:, bank, :].rearrange("p (j t) -> p j t", t=128)[:, :, 0:n]
                                cp(uext[:, c4:c4 + 4, sg['eu']:sg['eu'] + n], pv, [('ps', bank)], ['uext'], eng='act')
                        P.barrier()
                    with ExitStack() as ph2:
                        co = sb("co", [128, CT, TM], F32, ph2)
                        sqt = [sb("sqt", [128, TM], F32, ph2) for _ in range(2)]
                        mean = sb("mean", [128, TM], F32, ph2); rstd = sb("rstd", [128, TM], F32, ph2)
                        m2 = sb("m2", [128, TM], F32, ph2)
                        for pr in parts:
                            n, c0, si, ub = pr['n'], pr['c0'], pr['st'], pr['ub']
                            cs = slice(c0, c0 + n)
                            cp(uext[:, :, ub:ub + 30], utail[si][:, :, :], [('utail', si), 'uext'], ['uext'])
                            for j in range(31):
                                for ct in range(CT):
                                    src = uext[:, ct, ub + j:ub + j + n]
                                    if j == 0:
                                        ts(co[:, ct, cs], src, cw[:, ct, 0:1], cb[:, ct:ct + 1], ALU.mult, ALU.add,
                                           ['uext', 'cw', 'cb'], [('co', ct)])
                                    else:
                                        stt(co[:, ct, cs], src, cw[:, ct, j:j + 1], co[:, ct, cs], ALU.mult, ALU.add,
                                            ['uext', 'cw', ('co', ct)], [('co', ct)])
                            if n >= 30:
                                cp(utail[si][:, :, :], uext[:, :, ub + n:ub + n + 30], ['uext'], [('utail', si)])
                            b1 = P.bank(); b2 = P.bank()
                            for ct in range(CT):
                                mm(ps[:, b1, 0:n], ones_f, co[:, ct, cs], ct == 0, ct == CT - 1, ['cst', ('co', ct)], [('ps', b1)],
                                   signal=(ct == CT - 1))
                            for ct in range(CT):
                                sid = ('sqt', ct % 2)
                                act(sqt[ct % 2][:, 0:n], co[:, ct, cs], AF.Square, [('co', ct)], [sid])
                                mm(ps[:, b2, 0:n], ones_f, sqt[ct % 2][:, 0:n], ct == 0, ct == CT - 1, ['cst', sid], [('ps', b2)],
                                   signal=True)
                            allco = [('co', ct) for ct in range(CT)]
                            ts(mean[:, cs], ps[:, b1, 0:n], 1.0 / CC, None, ALU.mult, None, [('ps', b1)], ['mean'])
                            tt(m2[:, cs], mean[:, cs], mean[:, cs], ALU.mult, ['mean'], ['m2'])
                            stt(rstd[:, cs], ps[:, b2, 0:n], 1.0 / CC, m2[:, cs], ALU.mult, ALU.subtract, [('ps', b2), 'm2'], ['rstd'])
                            rsqrt(rstd[:, cs], rstd[:, cs], 1.0, ['rstd'], ['rstd'])
                            tt(co[:, :, cs], co[:, :, cs], bc(mean[:, cs], 1, CT), ALU.subtract, allco + ['mean'], allco)
                            tt(co[:, :, cs], co[:, :, cs], bc(rstd[:, cs], 1, CT), ALU.mult, allco + ['rstd'], allco)
                            tt(co[:, :, cs], co[:, :, cs], bc(lng[:, :], 2, n), ALU.mult, allco + ['lng'], allco)
                            tt(co[:, :, cs], co[:, :, cs], bc(lnb[:, :], 2, n), ALU.add, allco + ['lnb'], allco)
                            act(brT[1][:, :, cs], co[:, :, cs], AF.Silu, allco, [('brT', 1)])
                        P.barrier()

                with ExitStack() as ph:
                    qT = sb("qT", [128, AH, TM], BF16, ph)
                    biasb = sb("biasb", [128, AH, 8], F32, ph)
                    qTsf = sb("qTsf", [128, 4, RW], F32, ph)
                    Knew = sb("Knew", [128, 4, 8], BF16, ph)
                    Vnew = sb("Vnew", [128, 512], BF16, ph)
                    for sg in segs:
                        n, c0, i_, pos0 = sg['n'], sg['c0'], sg['i'], sg['pos0']
                        cs = slice(c0, c0 + n)
                        isp = sg['kind'] == 'p'
                        with ExitStack() as ph2:
                            qt_ = sb("qt_", [128, DM], F32, ph2); kt_ = sb("kt_", [128, 512], F32, ph2)
                            vt_ = sb("vt_", [128, 512], F32, ph2)
                            qTf = sb("qTf", [128, AH, 128], F32, ph2)
                            gate = sb("gate", [128, AH, 8], F32, ph2); top8 = sb("top8", [128, AH, 8], F32, ph2)
                            thr = sb("thr", [128, AH], F32, ph2)
                            t1 = sb("t1", [128, AH, 64], F32, ph2); t2 = sb("t2", [128, AH, 64], F32, ph2)
                            qrb = sb("qrb", [128, DM], BF16, ph2); krb = sb("krb", [128, 512], BF16, ph2)
                            csn = sb("csn", [128, 128], F32, ph2)
                            rp = rope_p[pos0:pos0 + n, :] if isp else rope_s[0:8, :]
                            dma(csn[:n, :], rp, [], ['csn'])
                            for cc0 in range(0, DM, 512):
                                def cons_q(sg_, bank, cc0=cc0):
                                    cp(qt_[:sg_['n'], cc0:cc0 + 512], ps[:sg_['n'], bank, 0:512], [('ps', bank)], ['qt_'], eng='act')
                                linear(W_in, 0, KC, c.oq + cc0, 512, hT, 'hT', [sg], cons_q)

                            def cons_k(sg_, bank):
                                cp(kt_[:sg_['n'], :], ps[:sg_['n'], bank, 0:512], [('ps', bank)], ['kt_'], eng='act')

                            def cons_v2(sg_, bank):
                                cp(vt_[:sg_['n'], :], ps[:sg_['n'], bank, 0:512], [('ps', bank)], ['vt_'], eng='act')
                            linear(W_in, 0, KC, c.ok, 512, hT, 'hT', [sg], cons_k)
                            linear(W_in, 0, KC, c.ov, 512, hT, 'hT', [sg], cons_v2)

                            def rope(src, nh, sid):
                                sv = src.rearrange("p (h two d) -> p h two d", two=2, d=64)
                                x1, x2 = sv[:, :, 0, :], sv[:, :, 1, :]
                                cosb = bc(csn[:n, 0:64], 1, nh); sinb = bc(csn[:n, 64:128], 1, nh)
                                a1, a2 = t1[:n, 0:nh, :], t2[:n, 0:nh, :]
                                tt(a1, x1, cosb, ALU.mult, [sid, 'csn'], ['t1'])
                                tt(a2, x2, sinb, ALU.mult, [sid, 'csn'], ['t2'])
                                tt(a1, a1, a2, ALU.subtract, ['t1', 't2'], ['t1'])
                                tt(a2, x1, sinb, ALU.mult, [sid, 'csn'], ['t2'])
                                tt(x2, x2, cosb, ALU.mult, [sid, 'csn'], [sid])
                                tt(x2, x2, a2, ALU.add, [sid, 't2'], [sid])
                                cp(x1, a1, ['t1', sid], [sid])
                            rope(qt_[:n, :], AH, 'qt_')
                            rope(kt_[:n, :], 4, 'kt_')
                            qr, kr = qt_, kt_
                            ko = k_p if isp else k_s
                            vo = v_p if isp else v_s
                            r0 = pos0 if isp else 0
                            dma(ko[l, r0:r0 + n, :], kr[:n, :], ['kt_'], [])
                            dma(vo[l, r0:r0 + n, :], vt_[:n, :], ['vt_'], [])
                            cp(qrb[:n, :], qr[:n, :], ['qt_'], ['qrb'])
                            cp(krb[:n, :], kr[:n, :], ['kt_'], ['krb'])
                            if isp:
                                cp(Vc[:n, pos0 // 128, :], vt_[:n, :], ['vt_'], ['Vc'])
                            else:
                                cp(Vnew[:n, :], vt_[:n, :], ['vt_'], ['Vnew'])
                            transpose_to(qT, 'qT', qrb, 'qrb', n, c0, AH)
                            bb = P.bbank()
                            for j in range(4):
                                tr(pb[:, bb, j * 128:j * 128 + n], krb[:n, j * 128:(j + 1) * 128], ident_b[:n, :n],
                                   ['krb', 'ident_b'], [('pb', bb)], signal=(j == 3))
                            pv = pb[:, bb, 0:512].rearrange("p (j t) -> p j t", t=128)[:, :, 0:n]
                            if isp:
                                cp(Kc[:, :, pos0:pos0 + n], pv, [('pb', bb)], ['Kc'], eng='act')
                            else:
                                cp(Knew[:, :, 0:n], pv, [('pb', bb)], ['Knew'], eng='act')
                            for c4 in range(0, AH, 4):
                                bank = P.bank()
                                for j in range(4):
                                    tr(ps[:, bank, j * 128:j * 128 + n], qr[:n, (c4 + j) * 128:(c4 + j + 1) * 128],
                                       ident_f[:n, :n], ['qt_', 'cst'], [('ps', bank)], signal=(j == 3))
                                pv = ps[:, bank, :].rearrange("p (j t) -> p j t", t=128)[:, :, 0:n]
                                cp(qTf[:, c4:c4 + 4, 0:n], pv, [('ps', bank)], ['qTf'], eng='act')
                            if isp:
                                bank = P.bank()
                                for kv in range(4):
                                    mm(ps[:, bank, kv:kv + 1], kr[:n, kv * 128:(kv + 1) * 128], ones_f[:n, 0:1], True, True,
                                       ['kt_', 'cst'], [('ps', bank)], signal=(kv == 3))
                                if (pos0 // 128) % 2 == 0:
                                    cp(khalf[:, :], ps[:, bank, 0:4], [('ps', bank)], ['khalf'])
                                else:
                                    tt(khalf[:, :], khalf[:, :], ps[:, bank, 0:4], ALU.add, ['khalf', ('ps', bank)], ['khalf'])
                                    ts(kmT[:, :, pos0 // 256], khalf[:, :], 1.0 / 256, None, ALU.mult, None, ['khalf'], ['kmT'])
                            b = pos0 // 256
                            if isp:
                                if b > 0:
                                    memset(gate[:n, :, :], -1e30, ['gate'])
                                    bank = P.bank()
                                    for h in range(AH):
                                        mm(ps[:n, bank, h * 8:h * 8 + b], qTf[:, h, 0:n], kmT[:, h // AR, 0:b], True, True,
                                           ['qTf', 'kmT'], [('ps', bank)], signal=(h == AH - 1))
                                    pvg = ps[:n, bank, 0:AH * 8].rearrange("p (h e) -> p h e", e=8)[:, :, 0:b]
                                    cp(gate[:n, :, 0:b], pvg, [('ps', bank)], ['gate'])
                                    for h in range(AH):
                                        P.op('dve', (lambda o_, i__: (lambda e: e.max(o_, i__)))(top8[:n, h, :], gate[:n, h, :]),
                                             ['gate'], ['top8'])
                                    ts(thr[:n, :], top8[:n, :, 2], -1e29, None, ALU.max, None, ['top8'], ['thr'])
                                    tt(biasb[:n, :, :], gate[:n, :, :], bc(thr[:n, :], 2, 8), ALU.is_ge, ['gate', 'thr'], ['biasb'])
                                    ts(biasb[:n, :, :], biasb[:n, :, :], -1.0, -NEG, ALU.add, ALU.mult, ['biasb'], ['biasb'])

                            else:
                                for g in range(4):
                                    cp(qTsf[:, g, :].rearrange("p (r q) -> p r q", q=8), qTf[:, g * AR:(g + 1) * AR, 0:8],
                                       ['qTf'], ['qTsf'])
                            P.barrier()

                        if isp:
                            with ExitStack() as ph2:
                                b = pos0 // 256
                                kend = pos0 + n
                                nkt = kend // 128
                                sm = sb("sm", [128, SEQ], F32, ph2); pe_ = sb("pe_", [128, SEQ], F32, ph2)
                                Pn = sb("Pn", [128, SEQ], BF16, ph2)
                                PTs = sb("PTs", [128, SEQ // 128, 128], BF16, ph2)
                                mx = sb("mx", [128, 4], F32, ph2)
                                for h in range(AH):
                                    kv = h // AR
                                    for k0 in range(0, kend, 512):
                                        kw = min(512, kend - k0)
                                        bank = P.bank()
                                        mm(ps[:n, bank, 0:kw], qT[:, h, cs], Kc[:, kv, k0:k0 + kw], True, True, ['qT', 'Kc'],
                                           [('ps', bank)], signal=True)
                                        for j0 in range(0, kw, 256):
                                            jb = (k0 + j0) // 256
                                            if jb < b:
                                                ts(sm[:n, k0 + j0:k0 + j0 + 256], ps[:n, bank, j0:j0 + 256], SCALE,
                                                   biasb[:n, h, jb:jb + 1], ALU.mult, ALU.add, [('ps', bank), 'biasb'], ['sm'])
                                            else:
                                                a0 = k0 + j0
                                                if a0 < pos0:
                                                    ts(sm[:n, a0:pos0], ps[:n, bank, j0:j0 + (pos0 - a0)], SCALE, None, ALU.mult,
                                                       None, [('ps', bank)], ['sm'])
                                                jj = pos0 - k0
                                                stt(sm[:n, pos0:kend], ps[:n, bank, jj:jj + n], SCALE, caus[:n, :n], ALU.mult,
                                                    ALU.add, [('ps', bank), 'cst'], ['sm'])
                                    red(mx[:n, 0:1], sm[:n, 0:kend], ALU.max, ['sm'], ['mx'])
                                    ts(mx[:n, 1:2], mx[:n, 0:1], -1.0, None, ALU.mult, None, ['mx'], ['mx'])
                                    act(pe_[:n, 0:kend], sm[:n, 0:kend], AF.Exp, ['sm', 'mx'], ['pe_', 'mx2'],
                                        bias=mx[:n, 1:2], scale=1.0, accum_out=mx[:n, 2:3])
                                    P.op('dve', (lambda o_, i__: (lambda e: e.reciprocal(o_, i__)))(mx[:n, 3:4], mx[:n, 2:3]),
                                         ['mx2'], ['mx3'])
                                    ts(Pn[:n, 0:kend], pe_[:n, 0:kend], mx[:n, 3:4], None, ALU.mult, None, ['pe_', 'mx3'], ['Pn'])
                                    for k8 in range(0, nkt, 8):
                                        m = min(8, nkt - k8)
                                        bb = P.bbank()
                                        for j in range(m):
                                            tr(pb[:, bb, j * 128:j * 128 + n], Pn[:n, (k8 + j) * 128:(k8 + j + 1) * 128],
                                               ident_b[:n, :n], ['Pn', 'ident_b'], [('pb', bb)], signal=(j == m - 1))
                                        pv = pb[:, bb, 0:m * 128].rearrange("p (j t) -> p j t", t=128)[:, :, 0:n]
                                        cp(PTs[:, k8:k8 + m, 0:n], pv, [('pb', bb)], ['PTs'], eng='act')
                                    bo = P.bank()
                                    for kt in range(nkt):
                                        mm(ps[:, bo, 0:n], Vc[:, kt, kv * 128:(kv + 1) * 128], PTs[:, kt, 0:n], kt == 0,
                                           kt == nkt - 1, ['Vc', 'PTs'], [('ps', bo)], signal=(kt == nkt - 1))
                                    cp(brT[2][:, h, cs], ps[:, bo, 0:n], [('ps', bo)], [('brT', 2)], eng='act')
                                P.barrier()
                        else:
                            with ExitStack() as ph2:
                                ckl, cvl = ck[l], cv[l]
                                qTs = sb("qTs", [128, 4, RW], BF16, ph2)
                                kp2 = sb("kp2", [128, 2, 512], F32, ph2); vp2 = sb("vp2", [128, 2, 512], F32, ph2)
                                vb2 = sb("vb2", [128, 2, 512], BF16, ph2)
                                KTc = sb("KTc", [128, 4, 256], BF16, ph2)
                                kmS = sb("kmS", [128, 4, NBS], F32, ph2)
                                gs = sb("gs", [128, 4, NBSP], F32, ph2); tp8 = sb("tp8", [128, 4, 8], F32, ph2)
                                thS = sb("thS", [128, 4], F32, ph2); bS = sb("bS", [128, 4, NBSP], F32, ph2)
                                smS = sb("smS", [128, 4, 256], F32, ph2); Pf = sb("Pf", [128, 4, 256], F32, ph2)
                                PT2 = sb("PT2", [128, 8, RW], BF16, ph2)
                                Oa = sb("Oa", [128, 4, 128], F32, ph2); On = sb("On", [128, 4, 128], BF16, ph2)
                                fm = sb("fm", [128, 4], F32, ph2); fl = sb("fl", [128, 4], F32, ph2)
                                fc = sb("fc", [128, 4], F32, ph2); fn_ = sb("fn_", [128, 4], F32, ph2)
                                fa = sb("fa", [128, 4], F32, ph2); frs = sb("frs", [128, 4], F32, ph2)
                                for g in range(4):
                                    cp(qTs[:, g, :].rearrange("p (r q) -> p r q", q=8), qT[:, g * AR:(g + 1) * AR, cs],
                                       ['qT'], ['qTs'])

                                def gather(dst, dst_id, src, j):
                                    P.dma('pool', lambda e: e.indirect_dma_start(
                                        out=dst, out_offset=None, in_=src[:, :],
                                        in_offset=bass.IndirectOffsetOnAxis(ap=idx[:, j:j + 1], axis=0)), ['idx'], [dst_id])
                                bk = P.bank()
                                for blk in range(NBS):
                                    for pg in range(2):
                                        gather(kp2[:, pg, :], ('kp2', pg), ckl, 2 * blk + pg)
                                    for kv in range(4):
                                        col = kv * NBS + blk
                                        for pg in range(2):
                                            mm(ps[:, bk, col:col + 1], kp2[:, pg, kv * 128:(kv + 1) * 128], ones_f[:, 0:1],
                                               pg == 0, pg == 1, [('kp2', pg), 'cst'], [('ps', bk)],
                                               signal=(kv == 3 and pg == 1))
                                ts(kmS[:, :, :], ps[:, bk, 0:4 * NBS].rearrange("p (k b) -> p k b", b=NBS), 1.0 / 256, None,
                                   ALU.mult, None, [('ps', bk)], ['kmS'])
                                memset(gs[:RW, :, :], -1e30, ['gs'])
                                bg = P.bank()
                                for g in range(4):
                                    mm(ps[:RW, bg, g * NBS:(g + 1) * NBS], qTsf[:, g, :], kmS[:, g, :], True, True,
                                       ['qTsf', 'kmS'], [('ps', bg)], signal=(g == 3))
                                cp(gs[:RW, :, 0:NBS], ps[:RW, bg, 0:4 * NBS].rearrange("p (k b) -> p k b", b=NBS),
                                   [('ps', bg)], ['gs'])
                                for g in range(4):
                                    P.op('dve', (lambda o_, i__: (lambda e: e.max(o_, i__)))(tp8[:RW, g, :], gs[:RW, g, :]),
                                         ['gs'], ['tp8'])
                                ts(thS[:RW, :], tp8[:RW, :, 2], -1e29, None, ALU.max, None, ['tp8'], ['thS'])
                                tt(bS[:RW, :, :], gs[:RW, :, :], bc(thS[:RW, :], 2, NBSP), ALU.is_ge, ['gs', 'thS'], ['bS'])
                                ts(bS[:RW, :, :], bS[:RW, :, :], -1.0, -NEG, ALU.add, ALU.mult, ['bS'], ['bS'])
                                memset(fm[:RW, :], -1e30, ['fm']); memset(fl[:RW, :], 0.0, ['fl'])
                                memset(Oa[:RW, :, :], 0.0, ['Oa'])

                                def online(width, nkt_, bias_ap, bias_ids, score_mm, pv_mm):
                                    banks = score_mm()
                                    for (bnk, g0, ng) in banks:
                                        pvs = ps[:RW, bnk, 0:ng * width].rearrange("p (g k) -> p g k", k=width)
                                        stt(smS[:RW, g0:g0 + ng, 0:width], pvs, SCALE, bias_ap(g0, ng), ALU.mult, ALU.add,
                                            [('ps', bnk)] + bias_ids, ['smS'])
                                    red(fc[:RW, :], smS[:RW, :, 0:width], ALU.max, ['smS'], ['fc'])
                                    tt(fn_[:RW, :], fm[:RW, :], fc[:RW, :], ALU.max, ['fm', 'fc'], ['fn_'])
                                    tt(fa[:RW, :], fm[:RW, :], fn_[:RW, :], ALU.subtract, ['fm', 'fn_'], ['fa'])
                                    act(fa[:RW, :], fa[:RW, :], AF.Exp, ['fa'], ['fa'])
                                    cp(fm[:RW, :], fn_[:RW, :], ['fn_'], ['fm'])
                                    tt(smS[:RW, :, 0:width], smS[:RW, :, 0:width], bc(fn_[:RW, :], 2, width), ALU.subtract,
                                       ['smS', 'fn_'], ['smS'])
                                    act(Pf[:RW, :, 0:width], smS[:RW, :, 0:width], AF.Exp, ['smS'], ['Pf'])
                                    red(frs[:RW, :], Pf[:RW, :, 0:width], ALU.add, ['Pf'], ['frs'])
                                    tt(fl[:RW, :], fl[:RW, :], fa[:RW, :], ALU.mult, ['fl', 'fa'], ['fl'])
                                    tt(fl[:RW, :], fl[:RW, :], frs[:RW, :], ALU.add, ['fl', 'frs'], ['fl'])
                                    kw_ = min(width, 128)
                                    bt = P.bank()
                                    for g in range(4):
                                        for kt in range(nkt_):
                                            col = (g * nkt_ + kt) * RW
                                            tr(ps[:kw_, bt, col:col + RW], Pf[:RW, g, kt * 128:kt * 128 + kw_], ident_f[:RW, :RW],
                                               ['Pf', 'cst'], [('ps', bt)], signal=(g == 3 and kt == nkt_ - 1))
                                    cp(PT2[:kw_, 0:4 * nkt_, :], ps[:kw_, bt, 0:4 * nkt_ * RW].rearrange("p (a r) -> p a r", r=RW),
                                       [('ps', bt)], ['PT2'], eng='act')
                                    bo = P.bank()
                                    pv_mm(bo, kw_)
                                    tt(Oa[:RW, :, :], Oa[:RW, :, :], bc(fa[:RW, :], 2, 128), ALU.mult, ['Oa', 'fa'], ['Oa'])
                                    tt(Oa[:RW, :, :], Oa[:RW, :, :], ps[:RW, bo, :].rearrange("p (g d) -> p g d", d=128), ALU.add,
                                       ['Oa', ('ps', bo)], ['Oa'])

                                for blk in range(NBS):
                                    for pg in range(2):
                                        gather(kp2[:, pg, :], ('kp2', pg), ckl, 2 * blk + pg)
                                        gather(vp2[:, pg, :], ('vp2', pg), cvl, 2 * blk + pg)
                                    cp(vb2[:, :, :], vp2[:, :, :], [('vp2', 0), ('vp2', 1)], ['vb2'], eng='act')
                                    for pg in range(2):
                                        bank = P.bank()
                                        for kv in range(4):
                                            tr(ps[:, bank, kv * 128:(kv + 1) * 128], kp2[:, pg, kv * 128:(kv + 1) * 128], ident_f,
                                               [('kp2', pg), 'cst'], [('ps', bank)], signal=(kv == 3))
                                        cp(KTc[:, :, pg * 128:(pg + 1) * 128], ps[:, bank, :].rearrange("p (k t) -> p k t", t=128),
                                           [('ps', bank)], ['KTc'], eng='act')

                                    def score_mm():
                                        out = []
                                        for g0 in (0, 2):
                                            bnk = P.bank()
                                            for gg in range(2):
                                                mm(ps[:RW, bnk, gg * 256:(gg + 1) * 256], qTs[:, g0 + gg, :], KTc[:, g0 + gg, :],
                                                   True, True, ['qTs', 'KTc'], [('ps', bnk)], signal=(gg == 1))
                                            out.append((bnk, g0, 2))
                                        return out

                                    def pv_mm(bo, kw_):
                                        for g in range(4):
                                            for kt in range(2):
                                                mm(ps[:RW, bo, g * 128:(g + 1) * 128], PT2[:, g * 2 + kt, :],
                                                   vb2[:, kt, g * 128:(g + 1) * 128], kt == 0, kt == 1, ['PT2', 'vb2'],
                                                   [('ps', bo)], signal=(g == 3 and kt == 1))
                                    online(256, 2, (lambda g0, ng, blk=blk: bc(bS[:RW, g0:g0 + ng, blk], 2, 256)), ['bS'],
                                           score_mm, pv_mm)

                                def score_mm2():
                                    bnk = P.bank()
                                    for g in range(4):
                                        mm(ps[:RW, bnk, g * 8:(g + 1) * 8], qTs[:, g, :], Knew[:, g, :], True, True,
                                           ['qTs', 'Knew'], [('ps', bnk)], signal=(g == 3))
                                    return [(bnk, 0, 4)]

                                def pv_mm2(bo, kw_):
                                    for g in range(4):
                                        mm(ps[:RW, bo, g * 128:(g + 1) * 128], PT2[:8, g, :], Vnew[:8, g * 128:(g + 1) * 128],
                                           True, True, ['PT2', 'Vnew'], [('ps', bo)], signal=(g == 3))
                                online(8, 1, (lambda g0, ng: bc(caus_s[:RW, 0:8], 1, ng)), ['cst'], score_mm2, pv_mm2)
                                P.op('dve', (lambda o_, i__: (lambda e: e.reciprocal(o_, i__)))(frs[:RW, :], fl[:RW, :]),
                                     ['fl'], ['frs'])
                                tt(On[:RW, :, :], Oa[:RW, :, :], bc(frs[:RW, :], 2, 128), ALU.mult, ['Oa', 'frs'], ['On'])
                                bb = P.bbank()
                                for g in range(4):
                                    tr(pb[:, bb, g * RW:(g + 1) * RW], On[:RW, g, :], ident_b[:RW, :RW], ['On', 'ident_b'],
                                       [('pb', bb)], signal=(g == 3))
                                for g in range(4):
                                    cp(brT[2][:, g * AR:(g + 1) * AR, cs], pb[:, bb, g * RW:(g + 1) * RW].rearrange(
                                        "p (r q) -> p r q", q=8), [('pb', bb)], [('brT', 2)], eng='act')
                                P.barrier()

                with ExitStack() as ph:
                    gsb = sb("gsb", [128, NSEG, 512], F32, ph); macc = sb("macc", [128, NSEG, 512], F32, ph)
                    tmpm = sb("tmpm", [128, NSEG, 512], F32, ph); mb = sb("mb", [128, NSEG, 512], BF16, ph)
                    wouts = [('ssd', l), ('conv', l), ('att', l)]
                    for cc0 in range(0, D, 512):
                        for bi in range(3):
                            def cons_gate(sg_, bank):
                                act(gsb[:sg_['n'], sg_['i'], :], ps[:sg_['n'], bank, 0:512], AF.Sigmoid, [('ps', bank)],
                                    [('gsb', sg_['i'])])

                            def cons_y(sg_, bank, bi=bi):
                                n_, i_ = sg_['n'], sg_['i']
                                if bi == 0:
                                    tt(macc[:n_, i_, :], ps[:n_, bank, 0:512], gsb[:n_, i_, :], ALU.mult,
                                       [('ps', bank), ('gsb', i_)], [('macc', i_)])
                                else:
                                    tt(tmpm[:n_, i_, :], ps[:n_, bank, 0:512], gsb[:n_, i_, :], ALU.mult,
                                       [('ps', bank), ('gsb', i_)], [('tmpm', i_)])
                                    tt(macc[:n_, i_, :], macc[:n_, i_, :], tmpm[:n_, i_, :], ALU.add,
                                       [('macc', i_), ('tmpm', i_)], [('macc', i_)])
                            linear(W_in, 0, KC, c.og + bi * D + cc0, 512, hT, 'hT', segs, cons_gate)
                            linear(wouts[bi], 0, KCM, cc0, 512, brT[bi], ('brT', bi), segs, cons_y)
                        for sg in segs:
                            n, i_ = sg['n'], sg['i']
                            cp(mb[:n, i_, :], macc[:n, i_, :], [('macc', i_)], [('mb', i_)])
                            bb = P.bbank()
                            for j in range(4):
                                tr(pb[:, bb, j * 128:j * 128 + n], mb[:n, i_, j * 128:(j + 1) * 128], ident_b[:n, :n],
                                   [('mb', i_), 'ident_b'], [('pb', bb)], signal=(j == 3))
                            pv = pb[:, bb, 0:512].rearrange("p (j t) -> p j t", t=128)[:, :, 0:n]
                            cp(mT[:, cc0 // 128:cc0 // 128 + 4, sg['c0']:sg['c0'] + n], pv, [('pb', bb)], ['mT'], eng='act')
                    P.barrier()
                brs.close()

                with ExitStack() as ph:
                    xt = sb("xt", [128, NSEG, D], F32, ph)
                    for sg in segs:
                        n = sg['n']
                        src = (xin_p[sg['pos0']:sg['pos0'] + n, :] if sg['kind'] == 'p' else xin_s[0:8, :])
                        dma(xt[:n, sg['i'], :], src, [], [('xt', sg['i'])])
                    for cc0 in range(0, D, 512):
                        def cons_o(sg_, bank, cc0=cc0):
                            n_, i_ = sg_['n'], sg_['i']
                            tt(xt[:n_, i_, cc0:cc0 + 512], xt[:n_, i_, cc0:cc0 + 512], ps[:n_, bank, 0:512], ALU.add,
                               [('xt', i_), ('ps', bank)], [('xt', i_)])
                        linear(('o', l), 0, KC, cc0, 512, mT, 'mT', segs, cons_o)

                    hb = sb("hb", [128, D], BF16, ph)
                    r1 = sb("r1", [128, NSEG, 512], F32, ph); fb = sb("fb", [128, NSEG, 512], BF16, ph)
                    for sg in segs:
                        rms_to_T(sg, xt[:sg['n'], sg['i'], :], ('xt', sg['i']), nffn, 'nffn', hb)
                    for q in range(4):
                        for cc0 in range(0, D, 512):
                            def cons_up(sg_, bank, cc0=cc0):
                                n_, i_ = sg_['n'], sg_['i']
                                act(r1[:n_, i_, :], ps[:n_, bank, 0:512], AF.Relu, [('ps', bank)], [('r1', i_)])
                                tt(fb[:n_, i_, :], r1[:n_, i_, :], r1[:n_, i_, :], ALU.mult, [('r1', i_)], [('fb', i_)])
                                bb = P.bbank()
                                for j in range(4):
                                    tr(pb[:, bb, j * 128:j * 128 + n_], fb[:n_, i_, j * 128:(j + 1) * 128], ident_b[:n_, :n_],
                                       [('fb', i_), 'ident_b'], [('pb', bb)], signal=(j == 3))
                                pv = pb[:, bb, 0:512].rearrange("p (j t) -> p j t", t=128)[:, :, 0:n_]
                                cp(mT[:, cc0 // 128:cc0 // 128 + 4, sg_['c0']:sg_['c0'] + n_], pv, [('pb', bb)], ['mT'], eng='act')
                            linear(('up', l), 0, KC, q * D + cc0, 512, hT, 'hT', segs, cons_up)
                        for cc0 in range(0, D, 512):
                            def cons_dn(sg_, bank, cc0=cc0):
                                n_, i_ = sg_['n'], sg_['i']
                                tt(xt[:n_, i_, cc0:cc0 + 512], xt[:n_, i_, cc0:cc0 + 512], ps[:n_, bank, 0:512], ALU.add,
                                   [('xt', i_), ('ps', bank)], [('xt', i_)])
                            linear(('down', l), q * KC, KC, cc0, 512, mT, 'mT', segs, cons_dn)
                    for sg in segs:
                        n, i_ = sg['n'], sg['i']
                        isp = sg['kind'] == 'p'
                        if l < L - 1:
                            dst = xres_p[sg['pos0']:sg['pos0'] + n, :] if isp else xres_s[0:8, :]
                            dma(dst, xt[:n, i_, :], [('xt', i_)], [])
                        else:
                            act(hb[:n, :], xt[:n, i_, :], AF.Square, [('xt', i_)], ['hb', 'st1'], accum_out=st1[:n, 0:1])
                            rsqrt(st1[:n, 1:2], st1[:n, 0:1], 1.0 / D, ['st1'], ['st1'])
                            ts(xt[:n, i_, :], xt[:n, i_, :], st1[:n, 1:2], None, ALU.mult, None, [('xt', i_), 'st1'], [('xt', i_)])
                            for cc0 in range(0, D, 512):
                                dma(r1[:n, 0, :], p_nfin[0:1, cc0:cc0 + 512].broadcast_to([n, 512]), [], [('r1', 0)])
                                tt(xt[:n, i_, cc0:cc0 + 512], xt[:n, i_, cc0:cc0 + 512], r1[:n, 0, :], ALU.mult,
                                   [('xt', i_), ('r1', 0)], [('xt', i_)])
                            dst = y_p[sg['pos0']:sg['pos0'] + n, :] if isp else y_s[0:8, :]
                            dma(dst, xt[:n, i_, :], [('xt', i_)], [])
                    P.barrier()
        P.barrier()
        P.emit()
    return nc


def make_consts(c):
    cst = np.zeros((128, 128 * 4 + 1 + 8), np.float32)
    i = np.arange(128)
    cst[:, 0:128] = np.eye(128, dtype=np.float32)
    cst[:, 128:256] = (i[:, None] <= i[None, :]).astype(np.float32)
    cst[:, 256:384] = np.where(i[None, :] <= i[:, None], 0.0, NEG)
    cst[:, 384:512] = 1.0
    cst[:, 512] = i
    cst[:, 513:521] = np.where(np.arange(8)[None, :] <= (i % 8)[:, None], 0.0, NEG)
    return cst


def rope_table(pos):
    half = 64
    inv = np.exp(np.float32(-math.log(10000.0)) * np.arange(half, dtype=np.float32) * np.float32(2.0) / np.float32(128))
    inv = inv.astype(np.float32)
    ang = (pos.astype(np.float32)[:, None] * inv[None, :]).astype(np.float32)
    return np.concatenate([np.cos(ang), np.sin(ang)], axis=1).astype(np.float32)


def featmajor(v, ncol_tiles):
    v = np.asarray(v)
    lead = v.shape[:-1]
    out = v.reshape(lead + (ncol_tiles, 128))
    out = np.moveaxis(out, -1, 0)
    out = np.moveaxis(out, -1, 1)
    return np.ascontiguousarray(out)


def run(inp, c, TSEG=1, ncores=8):
    L = c.L
    nc = build(c, TSEG)
    f = lambda a: np.ascontiguousarray(np.asarray(a, dtype=np.float32))
    shared = {
        "w_in": f(inp["w_in"]), "w_ssd": f(inp["w_ssd_out"]), "w_conv": f(inp["w_conv_out"]), "w_att": f(inp["w_att_out"]),
        "w_o": f(inp["w_o"]), "w_up": f(inp["w_up"]), "w_down": f(inp["w_down"]),
        "p_nmix": np.stack([featmajor(inp["norm_mix"][l], c.KC) for l in range(L)]),
        "p_nffn": np.stack([featmajor(inp["norm_ffn"][l], c.KC) for l in range(L)]),
        "p_nfin": f(inp["norm_final"]).reshape(1, c.D),
        "p_scw": np.stack([featmajor(inp["ssd_conv_w"][l], c.XT) for l in range(L)]),
        "p_scb": np.stack([featmajor(inp["ssd_conv_b"][l], c.XT) for l in range(L)]),
        "p_dtb": f(inp["ssd_dt_bias"]).reshape(L, 1, c.NH), "p_alog": f(inp["ssd_a_log"]).reshape(L, 1, c.NH),
        "p_sd": f(inp["ssd_d"]).reshape(L, 1, c.NH),
        "p_snorm": np.stack([featmajor(inp["ssd_norm"][l], c.KCM) for l in range(L)]),
        "p_cw": np.stack([featmajor(inp["conv_w"][l], c.CT) for l in range(L)]),
        "p_cb": np.stack([featmajor(inp["conv_b"][l], c.CT) for l in range(L)]),
        "p_lng": np.stack([featmajor(inp["conv_ln_g"][l], c.CT) for l in range(L)]),
        "p_lnb": np.stack([featmajor(inp["conv_ln_b"][l], c.CT) for l in range(L)]),
        "consts": make_consts(c),
        "rope_p": rope_table(np.arange(c.SEQ)), "rope_s": rope_table(c.PAST + np.arange(8)),
    }
    shared = {k: f(v) for k, v in shared.items()}
    for l in range(L):
        shared["ck%d" % l] = f(inp["cache_k"][l]).reshape(c.NPHYS * 128, 512)
        shared["cv%d" % l] = f(inp["cache_v"][l]).reshape(c.NPHYS * 128, 512)
    in_maps = []
    for core in range(ncores):
        m = dict(shared)
        m["xp"] = f(inp["x_prompt"][core % 4])
        m["xs"] = f(inp["x_sample"][core])
        m["pt"] = np.ascontiguousarray(np.asarray(inp["page_table"][core], dtype=np.int32).reshape(1, c.NPG))
        ss = np.asarray(inp["state_ssm"])[:, core]
        m["i_ssmT"] = f(ss.reshape(L, c.DM, 128).transpose(0, 2, 1))
        sc = np.asarray(inp["state_ssm_conv"])[:, core]
        m["i_xtail"] = f(np.stack([featmajor(sc[l], c.XT) for l in range(L)]))
        cvs = np.asarray(inp["state_conv"])[:, core]
        m["i_utail"] = f(np.stack([featmajor(cvs[l], c.CT) for l in range(L)]))
        m["i_convrows"] = f(cvs)
        in_maps.append(m)
    res = run_bass_kernel_spmd(nc, in_maps, core_ids=list(range(ncores)))
    R = res.results
    B = 4
    y_prompt = np.stack([R[b]["y_p"] for b in range(B)])
    y_sample = np.stack([R[b]["y_s"] for b in range(8)])
    kp = np.stack([R[b]["k_p"] for b in range(B)], axis=1).reshape(L, B, c.SEQ, 4, 128)
    vp = np.stack([R[b]["v_p"] for b in range(B)], axis=1).reshape(L, B, c.SEQ, 4, 128)
    ks = np.stack([R[b]["k_s"] for b in range(8)], axis=1).reshape(L, 8, 8, 4, 128)
    vs = np.stack([R[b]["v_s"] for b in range(8)], axis=1).reshape(L, 8, 8, 4, 128)

    def unT(a):
        return np.ascontiguousarray(a.transpose(0, 2, 1)).reshape(L, c.NH, 64, 128)
    ssm_p = np.stack([unT(R[b]["o_ssm_p"]) for b in range(B)], axis=1)
    ssm_s = np.stack([unT(R[b]["o_ssm_s"]) for b in range(8)], axis=1)
    scp = np.stack([R[b]["o_sconv_p"] for b in range(B)], axis=1)
    scs = np.stack([R[b]["o_sconv_s"] for b in range(8)], axis=1)
    cvp = np.stack([R[b]["o_conv_p"] for b in range(B)], axis=1)
    cvs_ = np.stack([R[b]["o_conv_s"] for b in range(8)], axis=1)
    outs = (y_prompt, y_sample, kp, vp, ks, vs, ssm_p, ssm_s, scp, scs, cvp, cvs_)
    return tuple(np.ascontiguousarray(o, dtype=np.float32) for o in outs)


def kernel(**inputs):
    return run(inputs, Cfg(), TSEG=2)
```
